# Optimizing a Trainium2 kernel written in Bass

```python
import math
import jax, jax.numpy as jnp
from jax import lax
import numpy as np

D_MODEL = 4096
BATCH = 4
SEQ = 2048
DEPTH = 4

N_META = 16
BLOCK_Q = 128
HEAD_DIM = 128
CONV_WIDTH = D_MODEL // 4
DIFF_HEADS = (3 * D_MODEL) // (8 * HEAD_DIM)
SB_HEADS = (3 * D_MODEL) // (8 * HEAD_DIM)
DIFF_WIDTH = DIFF_HEADS * HEAD_DIM
SB_WIDTH = SB_HEADS * HEAD_DIM
MIX_WIDTH = CONV_WIDTH + DIFF_WIDTH + SB_WIDTH
DIFF_QK_DIM = HEAD_DIM // 2
IN_WIDTH = 3 * CONV_WIDTH + 3 * DIFF_WIDTH + 3 * SB_WIDTH
SHORT_CONV_K = 3
FFN_CONV_K = 3
D_FF = ((8 * D_MODEL // 3 + 255) // 256) * 256
DN_ALPHA = (2 * DEPTH) ** 0.25
DN_BETA = (8 * DEPTH) ** -0.25
LN_EPS = 1e-5

kernel_name = "hymba_style_conv_diff_stickbreak_deepnorm"


def lambda_init(layer):
    return 0.8 - 0.6 * math.exp(-0.3 * layer)


def layer_norm(x, g, b):
    xf = x.astype(jnp.float32)
    mu = jnp.mean(xf, axis=-1, keepdims=True)
    var = jnp.mean(jnp.square(xf - mu), axis=-1, keepdims=True)
    y = (xf - mu) * lax.rsqrt(var + LN_EPS)
    return (y * g.astype(jnp.float32) + b.astype(jnp.float32)).astype(x.dtype)


def causal_dwconv(x, w):
    k_size = w.shape[0]
    length = x.shape[1]
    xp = jnp.pad(x, ((0, 0), (k_size - 1, 0), (0, 0)))
    return sum(w[k] * xp[:, k:k + length] for k in range(k_size))


def query_blocks(length):
    bounds = [(0, N_META)]
    start = N_META
    while start < length:
        end = min(start + BLOCK_Q, length)
        bounds.append((start, end))
        start = end
    return bounds


def diff_attention(q, k, v, lam_q1, lam_k1, lam_q2, lam_k2, norm_g, lam_init):
    bsz, length, n_heads = v.shape[:3]
    q = q.transpose(0, 2, 3, 1, 4)
    k = k.transpose(0, 2, 3, 1, 4)
    v = v.transpose(0, 2, 1, 3)
    lam = (jnp.exp(jnp.sum(lam_q1.astype(jnp.float32) * lam_k1.astype(jnp.float32)))
           - jnp.exp(jnp.sum(lam_q2.astype(jnp.float32) * lam_k2.astype(jnp.float32)))
           + lam_init)
    scale = DIFF_QK_DIM ** -0.5
    outs = []
    for s, e in query_blocks(length):
        scores = jnp.einsum('bhcqd,bhckd->bhcqk', q[:, :, :, s:e], k[:, :, :, :e]).astype(jnp.float32) * scale
        mask = jnp.arange(e)[None, :] <= jnp.arange(s, e)[:, None]
        probs = jax.nn.softmax(jnp.where(mask, scores, -jnp.inf), axis=-1)
        w = probs[:, :, 0] - lam * probs[:, :, 1]
        outs.append(jnp.einsum('bhqk,bhkd->bhqd', w.astype(v.dtype), v[:, :, :e]))
    o = jnp.concatenate(outs, axis=2).astype(jnp.float32)
    o = o * lax.rsqrt(jnp.mean(jnp.square(o), axis=-1, keepdims=True) + LN_EPS)
    o = (o * norm_g.astype(jnp.float32) * (1.0 - lam_init)).astype(v.dtype)
    return o.transpose(0, 2, 1, 3).reshape(bsz, length, n_heads * v.shape[-1])


def stick_breaking_attention(q, k, v):
    bsz, length, n_heads, hd = v.shape
    q = q.transpose(0, 2, 1, 3)
    k = k.transpose(0, 2, 1, 3)
    v = v.transpose(0, 2, 1, 3)
    scale = hd ** -0.5
    outs = []
    for s, e in query_blocks(length):
        z = jnp.einsum('bhqd,bhkd->bhqk', q[:, :, s:e], k[:, :, :e]).astype(jnp.float32) * scale
        mask = jnp.arange(e)[None, :] < jnp.arange(s, e)[:, None]
        log_keep = jnp.where(mask, jax.nn.log_sigmoid(-z), 0.0)
        tail = lax.cumsum(log_keep, axis=3, reverse=True) - log_keep
        weights = jnp.where(mask, jnp.exp(jax.nn.log_sigmoid(z) + tail), 0.0)
        outs.append(jnp.einsum('bhqk,bhkd->bhqd', weights.astype(v.dtype), v[:, :, :e]))
    o = jnp.concatenate(outs, axis=2)
    return o.transpose(0, 2, 1, 3).reshape(bsz, length, n_heads * hd)


def conv_glu_ffn(x, w_up, conv_w, w_down):
    u = causal_dwconv(x @ w_up, conv_w)
    gate, up = jnp.split(u, 2, axis=-1)
    return (jax.nn.silu(gate) * up) @ w_down


def setup_inputs(seed: int = 0) -> dict:
    key = jax.random.key(seed)
    ks = jax.random.split(key, 20)
    f32 = jnp.float32
    nrm = lambda k, shape, s: jax.random.normal(k, shape, f32) * s
    return {
        "x": nrm(ks[0], (BATCH, SEQ, D_MODEL), 1.0),
        "meta_tokens": nrm(ks[1], (N_META, D_MODEL), 1.0),
        "emb_ln_g": 1.0 + nrm(ks[2], (D_MODEL,), 0.02),
        "emb_ln_b": nrm(ks[3], (D_MODEL,), 0.02),
        "w_in": nrm(ks[4], (DEPTH, D_MODEL, IN_WIDTH), D_MODEL ** -0.5),
        "short_conv_w": nrm(ks[5], (DEPTH, SHORT_CONV_K, CONV_WIDTH), SHORT_CONV_K ** -0.5),
        "lambda_q1": nrm(ks[6], (DEPTH, DIFF_QK_DIM), 0.1),
        "lambda_k1": nrm(ks[7], (DEPTH, DIFF_QK_DIM), 0.1),
        "lambda_q2": nrm(ks[8], (DEPTH, DIFF_QK_DIM), 0.1),
        "lambda_k2": nrm(ks[9], (DEPTH, DIFF_QK_DIM), 0.1),
        "diff_norm_g": 1.0 + nrm(ks[10], (DEPTH, HEAD_DIM), 0.02),
        "w_out": nrm(ks[11], (DEPTH, MIX_WIDTH, D_MODEL), DN_BETA * MIX_WIDTH ** -0.5),
        "ln1_g": 1.0 + nrm(ks[12], (DEPTH, D_MODEL), 0.02),
        "ln1_b": nrm(ks[13], (DEPTH, D_MODEL), 0.02),
        "w_up": nrm(ks[14], (DEPTH, D_MODEL, 2 * D_FF), D_MODEL ** -0.5),
        "ffn_conv_w": nrm(ks[15], (DEPTH, FFN_CONV_K, 2 * D_FF), FFN_CONV_K ** -0.5),
        "w_down": nrm(ks[16], (DEPTH, D_FF, D_MODEL), DN_BETA * D_FF ** -0.5),
        "ln2_g": 1.0 + nrm(ks[17], (DEPTH, D_MODEL), 0.02),
        "ln2_b": nrm(ks[18], (DEPTH, D_MODEL), 0.02),
    }


def reference(x, meta_tokens, emb_ln_g, emb_ln_b, w_in, short_conv_w, lambda_q1, lambda_k1,
              lambda_q2, lambda_k2, diff_norm_g, w_out, ln1_g, ln1_b, w_up, ffn_conv_w, w_down,
              ln2_g, ln2_b):
    bsz = x.shape[0]
    meta = jnp.broadcast_to(meta_tokens[None].astype(x.dtype), (bsz, N_META, x.shape[-1]))
    h = layer_norm(jnp.concatenate([meta, x], axis=1), emb_ln_g, emb_ln_b)
    length = h.shape[1]
    sizes = [CONV_WIDTH] * 3 + [DIFF_WIDTH] * 3 + [SB_WIDTH] * 3
    split_idx = [int(i) for i in np.cumsum(sizes)[:-1]]
    for l in range(DEPTH):
        proj = h @ w_in[l]
        cb, cc, ch, dq, dk, dv, sq, sk, sv = jnp.split(proj, split_idx, axis=-1)
        y_conv = cb * causal_dwconv(cc * ch, short_conv_w[l])
        y_diff = diff_attention(
            dq.reshape(bsz, length, DIFF_HEADS, 2, DIFF_QK_DIM),
            dk.reshape(bsz, length, DIFF_HEADS, 2, DIFF_QK_DIM),
            dv.reshape(bsz, length, DIFF_HEADS, HEAD_DIM),
            lambda_q1[l], lambda_k1[l], lambda_q2[l], lambda_k2[l], diff_norm_g[l], lambda_init(l))
        y_sb = stick_breaking_attention(
            sq.reshape(bsz, length, SB_HEADS, HEAD_DIM),
            sk.reshape(bsz, length, SB_HEADS, HEAD_DIM),
            sv.reshape(bsz, length, SB_HEADS, HEAD_DIM))
        mix = jnp.concatenate([y_conv, y_diff, y_sb], axis=-1) @ w_out[l]
        h = layer_norm(DN_ALPHA * h + mix, ln1_g[l], ln1_b[l])
        h = layer_norm(DN_ALPHA * h + conv_glu_ffn(h, w_up[l], ffn_conv_w[l], w_down[l]), ln2_g[l], ln2_b[l])
    return h[:, N_META:]
```

```python
import math
from contextlib import ExitStack

import numpy as np
import ml_dtypes
import concourse.bass as bass
import concourse.mybir as mybir
from concourse.bass_utils import run_bass_kernel_spmd

F32 = mybir.dt.float32
BF16 = mybir.dt.bfloat16
AF = mybir.ActivationFunctionType
ALU = mybir.AluOpType
AX = mybir.AxisListType

N_META = 16
LN_EPS = 1e-5
SAME_ENGINE_SYNC = True
DMA_POOL = 8


class Cfg:
    def __init__(self, D=4096, SEQ=2048, DEPTH=4, NPAIR=1, B=4, SPC=1):
        self.D, self.SEQ, self.DEPTH, self.NPAIR, self.B, self.SPC = D, SEQ, DEPTH, NPAIR, B, SPC
        assert NPAIR == 1 and B % SPC == 0
        self.L = N_META + SEQ
        self.NB = 2
        t = -(-self.L // 128) * 128
        if (t // 2) % 64:
            t += 128
        self.T = t
        self.TB = t // 2
        self.NBL = self.NB // NPAIR
        self.KC = D // 128
        self.CONV = D // 4
        self.NDH = (3 * D) // (8 * 128)
        self.NSH = self.NDH
        self.DFF = ((8 * D // 3 + 255) // 256) * 256
        self.FC = self.DFF // 128
        self.CCH = self.CONV // 128 // NPAIR
        self.HD = self.NDH // NPAIR
        self.HS = self.NSH // NPAIR
        self.NCL = 3 * self.CCH + 3 * self.HD + 3 * self.HS
        self.MCL = self.CCH + self.HD + self.HS
        self.alpha = (2 * DEPTH) ** 0.25
        assert self.MCL * NPAIR == self.KC
        self.mtiles = [(o, min(128, self.TB - o)) for o in range(0, self.TB, 128)]
        self.tsubs = [(o, min(512, self.TB - o)) for o in range(0, self.TB, 512)]
        nparts = -(-self.FC // self.KC)
        base = self.FC // nparts
        rem = self.FC % nparts
        self.fparts = []
        o = 0
        for i in range(nparts):
            n = base + (1 if i < rem else 0)
            self.fparts.append((o, n))
            o += n

    def lam_init(self, layer):
        return 0.8 - 0.6 * math.exp(-0.3 * layer)

    def in_cols(self, r):
        C, DW = self.CONV, self.NDH * 128
        cl, dl, sl = self.CCH * 128, self.HD * 128, self.HS * 128
        cols = []
        for i in range(3):
            cols.append(np.arange(i * C + r * cl, i * C + (r + 1) * cl))
        for i in range(3):
            cols.append(np.arange(3 * C + i * DW + r * dl, 3 * C + i * DW + (r + 1) * dl))
        for i in range(3):
            cols.append(np.arange(3 * C + 3 * DW + i * DW + r * sl, 3 * C + 3 * DW + i * DW + (r + 1) * sl))
        return np.concatenate(cols)

    def mix_rows(self):
        C, DW = self.CONV, self.NDH * 128
        cl, dl, sl = self.CCH * 128, self.HD * 128, self.HS * 128
        rows = []
        for r in range(self.NPAIR):
            rows.append(np.arange(r * cl, (r + 1) * cl))
            rows.append(np.arange(C + r * dl, C + (r + 1) * dl))
            rows.append(np.arange(C + DW + r * sl, C + DW + (r + 1) * sl))
        return np.concatenate(rows)


class Buf:
    __slots__ = ("name", "w", "r")

    def __init__(self, name):
        self.name = name
        self.w = None
        self.r = {}


class Eng:
    def __init__(self, name, h, sem, sync_self):
        self.name, self.h, self.sem, self.cnt = name, h, sem, 0
        self.waited = {}
        self.sync_self = sync_self


class DmaQ:
    def __init__(self, eng, sems):
        self.eng, self.sems, self.idx = eng, sems, 0


class Sched:
    def __init__(self, nc, es):
        self.nc = nc
        mk = lambda n: es.enter_context(nc.semaphore(n))
        self.pe = Eng("pe", nc.tensor, mk("s_pe"), False)
        self.act = Eng("act", nc.scalar, mk("s_act"), SAME_ENGINE_SYNC)
        self.dve = Eng("dve", nc.vector, mk("s_dve"), SAME_ENGINE_SYNC)
        self.pool = Eng("pool", nc.gpsimd, mk("s_pool"), SAME_ENGINE_SYNC)
        self.sp = Eng("sp", nc.sync, mk("s_sp"), False)
        self.engs = [self.pe, self.act, self.dve, self.pool, self.sp]
        self.q_sync = DmaQ(self.sp, [mk(f"s_dq{i}") for i in range(DMA_POOL)])
        self.q_pool = DmaQ(self.pool, [mk(f"s_dp{i}") for i in range(DMA_POOL)])
        self.queues = [self.q_sync, self.q_pool]
        self.n_inst = 0

    def _deps(self, reads, writes):
        d = {}

        def add(ev):
            if ev is not None:
                s, v = ev
                if d.get(s, (None, 0))[1] < v:
                    d[s] = (s, v)

        for b in reads:
            add(b.w)
        for b in writes:
            add(b.w)
            for s, v in b.r.items():
                add((s, v))
        return list(d.values())

    def _wait(self, eng, deps):
        for s, v in deps:
            if s is eng.sem and not eng.sync_self:
                continue
            if eng.waited.get(s, 0) < v:
                eng.h.wait_ge(s, v)
                eng.waited[s] = v
                self.n_inst += 1

    def _commit(self, ev, reads, writes):
        s, v = ev
        for b in writes:
            b.w = ev
            b.r = {}
        for b in reads:
            if b.r.get(s, 0) < v:
                b.r[s] = v

    def op(self, eng, emit, reads=(), writes=()):
        self._wait(eng, self._deps(reads, writes))
        inst = emit(eng.h)
        eng.cnt += 1
        inst.then_inc(eng.sem, 1)
        self.n_inst += 1
        self._commit((eng.sem, eng.cnt), reads, writes)

    def chain(self, eng, emits, reads=(), writes=()):
        for e in emits:
            self.op(eng, e, reads=reads, writes=writes)

    def dma(self, q, out, in_, reads=(), writes=()):
        slot, rnd = q.idx % len(q.sems), q.idx // len(q.sems)
        q.idx += 1
        sem = q.sems[slot]
        deps = self._deps(reads, writes)
        if rnd > 0:
            deps.append((sem, 16 * rnd))
        self._wait(q.eng, deps)
        q.eng.h.dma_start(out=out, in_=in_).then_inc(sem, 16)
        self.n_inst += 1
        self._commit((sem, 16 * (rnd + 1)), reads, writes)

    def all_events(self):
        evs = [(e.sem, e.cnt) for e in self.engs if e.cnt > 0]
        for q in self.queues:
            for i, s in enumerate(q.sems):
                n = (q.idx - i + len(q.sems) - 1) // len(q.sems)
                if n > 0:
                    evs.append((s, 16 * n))
        return evs

    def barrier(self, engs=None):
        evs = self.all_events()
        for e in (engs or self.engs):
            sv = e.sync_self
            e.sync_self = True
            self._wait(e, evs)
            e.sync_self = sv


class Builder:
    def __init__(self, cfg: Cfg, debug=False):
        self.debug = debug
        self.c = cfg
        self.nc = bass.Bass("TRN2", target_bir_lowering=False)
        self._uid = 0

    def uid(self, p):
        self._uid += 1
        return f"{p}{self._uid}"

    def sb(self, es, shape, dt, name=None):
        return es.enter_context(self.nc.sbuf_tensor(self.uid(name or "sb"), list(shape), dt))

    def ps(self, es, shape, dt, name=None):
        return es.enter_context(self.nc.psum_tensor(self.uid(name or "ps"), list(shape), dt))

    def dram(self, name, shape, dt, kind="Internal"):
        if kind == "Internal" and self.debug:
            kind = "ExternalOutput"
        return self.nc.dram_tensor(name, list(shape), dt, kind=kind).ap()

    def build(self):
        c, nc = self.c, self.nc
        D, T, TB, KC, DEPTH = c.D, c.T, c.TB, c.KC, c.DEPTH
        R = c.NBL * TB
        ein = lambda n, s: self.dram(n, s, F32, kind="ExternalInput")
        self.tok = ein("tok", [c.SPC * R, D])
        self.embg = ein("embg", [1, D])
        self.embb = ein("embb", [1, D])
        self.w_in = ein("w_in", [DEPTH * D, c.NCL * 128])
        self.scw = ein("scw", [DEPTH * 128, c.CCH * 3])
        self.lamv = ein("lamv", [DEPTH * 4, 64])
        self.dng = ein("dng", [DEPTH, 128])
        self.w_out = ein("w_out", [DEPTH * D, D])
        self.ln1g = ein("ln1g", [DEPTH, D])
        self.ln1b = ein("ln1b", [DEPTH, D])
        self.w_up = ein("w_up", [DEPTH * D, 2 * c.DFF])
        self.fcw = ein("fcw", [DEPTH * 128, 2 * c.FC * 3])
        self.w_down = ein("w_down", [DEPTH * c.DFF, D])
        self.ln2g = ein("ln2g", [DEPTH, D])
        self.ln2b = ein("ln2b", [DEPTH, D])
        self.out = self.dram("out", [c.SPC * R, D], F32, kind="ExternalOutput")
        self.hres = self.dram("hres", [R, D], F32)
        self.ypre = self.dram("ypre", [R, D], F32)
        self.hnT = self.dram("hnT", [c.NB * KC * 128, TB], BF16)
        self.qkvT = self.dram("qkvT", [(c.NCL - 3 * c.CCH) * 128, T], BF16)
        self.cvT = self.dram("cvT", [3 * c.CCH * 128, T], F32)
        self.mixT = self.dram("mixT", [c.NB * c.MCL * 128, TB], BF16)
        self.aT = self.dram("aT", [c.FC * 128, TB], BF16)

        with ExitStack() as es:
            self.s = Sched(nc, es)
            self.consts(es)
            self.s.barrier()
            for sq in range(c.SPC):
                self.sq = sq
                self.stage_embed()
                for l in range(DEPTH):
                    self.stage_win(l)
                    self.stage_mixers(l)
                    self.stage_dense(l)
            self.s.barrier()
        return nc

    def consts(self, es):
        s, c = self.s, self.c
        self.ident = self.sb(es, [128, 128], BF16, "ident")
        self.mask_le = self.sb(es, [128, 128], F32, "mle")
        self.mask_lt = self.sb(es, [128, 128], F32, "mlt")
        self.ones = self.sb(es, [128, c.T], F32, "ones")
        self.b_const = Buf("const")
        g = self.nc.gpsimd

        sel = lambda t, op: (lambda h: h.affine_select(out=t[:], in_=t[:], pattern=[[-1, 128]], compare_op=op,
                                                       fill=0.0, base=0, channel_multiplier=1))
        s.chain(s.pool, [
            lambda h: h.memset(self.ident[:], 1.0), sel(self.ident, ALU.is_equal),
            lambda h: h.memset(self.mask_le[:], 1.0), sel(self.mask_le, ALU.is_ge),
            lambda h: h.memset(self.mask_lt[:], 1.0), sel(self.mask_lt, ALU.is_gt),
            lambda h: h.memset(self.ones[:], 1.0),
        ], writes=[self.b_const])

    def ln_pass(self, es, src, g_ap, b_ap, XT, b_XT, blk_local, h_dst=None, hnT_dst=None, src_off=0, dst_off=0):
        s, c, nc = self.s, self.c, self.nc
        D, KC, TB = c.D, c.KC, c.TB
        with ExitStack() as st:
            gt = self.sb(st, [128, D], F32, "lng")
            bt = self.sb(st, [128, D], F32, "lnb")
            b_gb = Buf("gb")
            s.dma(s.q_sync, gt[:], g_ap.partition_broadcast(128), writes=[b_gb])
            s.dma(s.q_sync, bt[:], b_ap.partition_broadcast(128), writes=[b_gb])
            NX = 2
            xt = [self.sb(st, [128, D], F32, "lnx") for _ in range(NX)]
            hb = [self.sb(st, [128, D], BF16, "lnhb") for _ in range(NX)]
            st4 = [self.sb(st, [128, 8], F32, "lnst") for _ in range(NX)]
            junk = self.sb(st, [128, D], BF16, "lnjunk")
            b_x = [Buf("lnx") for _ in range(NX)]
            b_hb = [Buf("lnhb") for _ in range(NX)]
            b_st = [Buf("lnst") for _ in range(NX)]
            b_junk = Buf("junk")
            pt = [self.ps(st, [128, 8, 128], BF16, "lnpt") for _ in range(2)]
            b_pt = [Buf("lnpt") for _ in range(2)]
            pti = 0
            for ti, (off, n) in enumerate(c.mtiles):
                i = ti % NX
                r0 = blk_local * TB + off
                X, HB, ST = xt[i], hb[i], st4[i]
                s.dma(s.q_sync, X[:n, :], src[src_off + r0:src_off + r0 + n, :], writes=[b_x[i]])
                s.op(s.dve, lambda h: h.memset(ST[:n, :], 0.0), writes=[b_st[i]])
                s.op(s.act, lambda h: h.activation(out=junk[:n, :], in_=X[:n, :], func=AF.Identity,
                                                   accum_out=ST[:n, 0:1]),
                     reads=[b_x[i]], writes=[b_junk, b_st[i]])
                s.op(s.act, lambda h: h.activation(out=junk[:n, :], in_=X[:n, :], func=AF.Square,
                                                   accum_out=ST[:n, 1:2]),
                     reads=[b_x[i]], writes=[b_junk, b_st[i]])

                s.chain(s.dve, [
                    lambda h: h.tensor_scalar(out=ST[:n, 2:3], in0=ST[:n, 0:1], scalar1=1.0 / D, scalar2=None, op0=ALU.mult),
                    lambda h: h.tensor_tensor(out=ST[:n, 4:5], in0=ST[:n, 2:3], in1=ST[:n, 2:3], op=ALU.mult),
                    lambda h: h.scalar_tensor_tensor(out=ST[:n, 5:6], in0=ST[:n, 1:2], scalar=1.0 / D, in1=ST[:n, 4:5],
                                                     op0=ALU.mult, op1=ALU.subtract),
                    lambda h: h.tensor_scalar(out=ST[:n, 5:6], in0=ST[:n, 5:6], scalar1=LN_EPS, scalar2=None, op0=ALU.add),
                ], writes=[b_st[i]])
                s.chain(s.act, [
                    lambda h: h.activation(out=ST[:n, 6:7], in_=ST[:n, 5:6], func=AF.Ln),
                    lambda h: h.activation(out=ST[:n, 6:7], in_=ST[:n, 6:7], func=AF.Exp, scale=-0.5),
                ], writes=[b_st[i]])
                s.op(s.dve, lambda h: h.scalar_tensor_tensor(out=ST[:n, 7:8], in0=ST[:n, 2:3], scalar=-1.0,
                                                             in1=ST[:n, 6:7], op0=ALU.mult, op1=ALU.mult),
                     writes=[b_st[i]])
                s.op(s.act, lambda h: h.activation(out=X[:n, :], in_=X[:n, :], func=AF.Identity,
                                                   scale=ST[:n, 6:7], bias=ST[:n, 7:8]),
                     reads=[b_st[i]], writes=[b_x[i]])
                s.op(s.dve, lambda h: h.tensor_tensor(out=X[:n, :], in0=X[:n, :], in1=gt[:n, :], op=ALU.mult),
                     reads=[b_gb], writes=[b_x[i]])
                s.op(s.dve, lambda h: h.tensor_tensor(out=X[:n, :], in0=X[:n, :], in1=bt[:n, :], op=ALU.add),
                     reads=[b_gb], writes=[b_x[i]])
                if h_dst is not None:
                    s.dma(s.q_sync, h_dst[dst_off + r0:dst_off + r0 + n, :], X[:n, :], reads=[b_x[i]])
                if XT is not None:
                    s.op(s.pool, lambda h: h.tensor_copy(out=HB[:n, :], in_=X[:n, :]), reads=[b_x[i]], writes=[b_hb[i]])
                    for k0 in range(0, KC, 8):
                        kn = min(8, KC - k0)
                        P, bP = pt[pti % 2], b_pt[pti % 2]
                        pti += 1

                        def tr(h):
                            last = None
                            for k in range(kn):
                                last = h.transpose(P[:, k, :n], HB[:n, (k0 + k) * 128:(k0 + k + 1) * 128],
                                                   self.ident[:n, :n])
                            return last

                        s.op(s.pe, tr, reads=[b_hb[i], self.b_const], writes=[bP])
                        eng = s.act if (k0 // 8) % 2 == 0 else s.dve
                        if eng is s.act:
                            s.op(eng, lambda h: h.copy(out=XT[:, k0:k0 + kn, off:off + n], in_=P[:, :kn, :n]),
                                 reads=[bP], writes=[b_XT])
                        else:
                            s.op(eng, lambda h: h.tensor_copy(out=XT[:, k0:k0 + kn, off:off + n], in_=P[:, :kn, :n]),
                                 reads=[bP], writes=[b_XT])
            if hnT_dst is not None:
                s.dma(s.q_sync, hnT_dst.rearrange("(k p) t -> p k t", p=128), XT[:, :KC, :TB], reads=[b_XT])
            s.barrier()

    def stage_embed(self):
        s, c = self.s, self.c
        with ExitStack() as es:
            XT = self.sb(es, [128, c.KC, c.TB], BF16, "XTe")
            b_XT = Buf("XT")
            for bl in range(c.NBL):
                blk = bl
                dst = self.hnT[blk * c.KC * 128:(blk + 1) * c.KC * 128, :]
                self.ln_pass(es, self.tok, self.embg[0:1, :], self.embb[0:1, :], XT, b_XT, bl,
                             h_dst=self.hres, hnT_dst=dst, src_off=self.sq * c.NBL * c.TB)

    def load_w(self, wbuf3, bW, Wd, row0, nk, col0, ncols):
        s = self.s
        for k0 in range(0, nk, 8):
            kn = min(8, nk - k0)
            src = Wd[row0 + k0 * 128: row0 + (k0 + kn) * 128, col0:col0 + ncols].rearrange("(k p) c -> p k c", p=128)
            s.dma(s.q_pool, wbuf3[:, k0:k0 + kn, :ncols], src, writes=[bW])

    def stage_win(self, l):
        s, c = self.s, self.c
        D, KC, TB, T = c.D, c.KC, c.TB, c.T
        NG = -(-c.NCL // 4)
        with ExitStack() as es:
            XT = self.sb(es, [128, KC, TB], BF16, "XT1")
            b_XT = Buf("XT")
            wb = [self.sb(es, [128, KC, 512], BF16, "w1") for _ in range(2)]
            b_w = [Buf("w") for _ in range(2)]
            pb = [self.ps(es, [128, 512], F32, "p1") for _ in range(6)]
            b_p = [Buf("p") for _ in range(6)]
            NS = 4
            stb = [self.sb(es, [128, TB], BF16, "st1b") for _ in range(NS)]
            stf = [self.sb(es, [128, TB], F32, "st1f") for _ in range(NS)]
            b_st = [Buf("st") for _ in range(NS)]
            pi = 0
            si = 0
            gi = 0
            for blk in range(c.NB):
                s.dma(s.q_sync, XT[:, :, :], self.hnT[blk * KC * 128:(blk + 1) * KC * 128, :]
                      .rearrange("(k p) t -> p k t", p=128), writes=[b_XT])
                for g in range(NG):
                    W, bW = wb[gi % 2], b_w[gi % 2]
                    gi += 1
                    ncols = min(512, c.NCL * 128 - g * 512)
                    self.load_w(W, bW, self.w_in, l * D, KC, g * 512, ncols)
                    for j in range(ncols // 128):
                        ch = g * 4 + j
                        isconv = ch < 3 * c.CCH
                        ST = (stf if isconv else stb)[si % NS]
                        bS = b_st[si % NS]
                        si += 1
                        for (o, n) in c.tsubs:
                            P, bP = pb[pi % 6], b_p[pi % 6]
                            pi += 1

                            def mm(h):
                                last = None
                                for k in range(KC):
                                    last = h.matmul(P[:, :n], lhsT=W[:, k, j * 128:(j + 1) * 128], rhs=XT[:, k, o:o + n],
                                                    start=(k == 0), stop=(k == KC - 1))
                                return last

                            s.op(s.pe, mm, reads=[bW, b_XT], writes=[bP])
                            s.op(s.act, lambda h: h.copy(out=ST[:, o:o + n], in_=P[:, :n]), reads=[bP], writes=[bS])
                        if isconv:
                            dst = self.cvT[ch * 128:(ch + 1) * 128, blk * TB:(blk + 1) * TB]
                        else:
                            q = ch - 3 * c.CCH
                            dst = self.qkvT[q * 128:(q + 1) * 128, blk * TB:(blk + 1) * TB]
                        s.dma(s.q_sync, dst, ST[:, :], reads=[bS])
            s.barrier()

    def mix_store(self, OT, bO, chunk):
        s, c = self.s, self.c
        for blk in range(c.NB):
            r = (blk * c.MCL + chunk) * 128
            s.dma(s.q_sync, self.mixT[r:r + 128, :], OT[:, blk * c.TB:(blk + 1) * c.TB], reads=[bO])

    def stage_mixers(self, l):
        s, c, nc = self.s, self.c, self.nc
        T = c.T
        NT = T // 128
        with ExitStack() as es:
            wt = self.sb(es, [128, c.CCH * 3], F32, "scw")
            b_wt = Buf("scw")
            s.dma(s.q_sync, wt[:], self.scw[l * 128:(l + 1) * 128, :], writes=[b_wt])
            NX = 2
            cb = [self.sb(es, [128, T], F32, "cb") for _ in range(NX)]
            cc = [self.sb(es, [128, T], F32, "cc") for _ in range(NX)]
            zb = [self.sb(es, [128, T + 2], F32, "zb") for _ in range(NX)]
            yb = [self.sb(es, [128, T], F32, "yb") for _ in range(NX)]
            ob = [self.sb(es, [128, T], BF16, "ob") for _ in range(NX)]
            b_cb = [Buf("cb") for _ in range(NX)]
            b_cc = [Buf("cc") for _ in range(NX)]
            b_zb = [Buf("zb") for _ in range(NX)]
            b_yb = [Buf("yb") for _ in range(NX)]
            b_ob = [Buf("ob") for _ in range(NX)]
            for ch in range(c.CCH):
                i = ch % NX
                CB, CC, ZB, YB, OB = cb[i], cc[i], zb[i], yb[i], ob[i]
                s.dma(s.q_sync, CB[:], self.cvT[ch * 128:(ch + 1) * 128, :], writes=[b_cb[i]])
                s.dma(s.q_sync, CC[:], self.cvT[(c.CCH + ch) * 128:(c.CCH + ch + 1) * 128, :], writes=[b_cc[i]])
                s.dma(s.q_sync, ZB[:, 2:], self.cvT[(2 * c.CCH + ch) * 128:(2 * c.CCH + ch + 1) * 128, :],
                      writes=[b_zb[i]])

                s.chain(s.dve, [
                    lambda h: h.memset(ZB[:, 0:2], 0.0),
                    lambda h: h.tensor_tensor(out=ZB[:, 2:], in0=ZB[:, 2:], in1=CC[:], op=ALU.mult),
                    lambda h: h.tensor_scalar(out=YB[:], in0=ZB[:, 2:], scalar1=wt[:, ch * 3 + 2:ch * 3 + 3], scalar2=None,
                                              op0=ALU.mult),
                    lambda h: h.scalar_tensor_tensor(out=YB[:], in0=ZB[:, 1:T + 1], scalar=wt[:, ch * 3 + 1:ch * 3 + 2],
                                                     in1=YB[:], op0=ALU.mult, op1=ALU.add),
                    lambda h: h.scalar_tensor_tensor(out=YB[:], in0=ZB[:, 0:T], scalar=wt[:, ch * 3:ch * 3 + 1],
                                                     in1=YB[:], op0=ALU.mult, op1=ALU.add),
                    lambda h: h.tensor_tensor(out=OB[:], in0=YB[:], in1=CB[:], op=ALU.mult),
                ], reads=[b_cb[i], b_cc[i], b_wt], writes=[b_zb[i], b_yb[i], b_ob[i]])
                self.mix_store(OB, b_ob[i], ch)
            s.barrier()
        with ExitStack() as es:
            self.attn_heads(es, l)
            s.barrier()

    def attn_heads(self, es, l):
        s, c, nc = self.s, self.c, self.nc
        T = c.T
        NT = T // 128
        lam_init = c.lam_init(l)
        lq = self.sb(es, [128, 4, 64], F32, "lq")
        lam = self.sb(es, [128, 8], F32, "lam")
        gt = self.sb(es, [128, 128], F32, "dng")
        b_par = Buf("par")
        for i in range(4):
            s.dma(s.q_sync, lq[:, i, :], self.lamv[l * 4 + i:l * 4 + i + 1, :].partition_broadcast(128), writes=[b_par])
        s.dma(s.q_sync, gt[:], self.dng[l:l + 1, :].partition_broadcast(128), writes=[b_par])

        def lam_mul(h):
            h.tensor_tensor(out=lq[:, 0, :], in0=lq[:, 0, :], in1=lq[:, 1, :], op=ALU.mult)
            return h.tensor_tensor(out=lq[:, 2, :], in0=lq[:, 2, :], in1=lq[:, 3, :], op=ALU.mult)

        def lam_red(h):
            h.reduce_sum(out=lam[:, 0:1], in_=lq[:, 0, :], axis=AX.X)
            return h.reduce_sum(out=lam[:, 1:2], in_=lq[:, 2, :], axis=AX.X)

        s.chain(s.dve, [lam_mul, lam_red], writes=[b_par])
        s.op(s.act, lambda h: h.activation(out=lam[:, 2:4], in_=lam[:, 0:2], func=AF.Exp), writes=[b_par])
        s.chain(s.dve, [
            lambda h: h.tensor_tensor(out=lam[:, 4:5], in0=lam[:, 2:3], in1=lam[:, 3:4], op=ALU.subtract),
            lambda h: h.tensor_scalar(out=lam[:, 5:6], in0=lam[:, 4:5], scalar1=lam_init, scalar2=-1.0, op0=ALU.add,
                                      op1=ALU.mult),
            lambda h: h.tensor_scalar(out=gt[:], in0=gt[:], scalar1=1.0 - lam_init, scalar2=None, op0=ALU.mult),
        ], writes=[b_par])
        neglam = lam[:, 5:6]

        NX = 2
        qT = [self.sb(es, [128, T], BF16, "qT") for _ in range(NX)]
        kT = [self.sb(es, [128, T], BF16, "kT") for _ in range(NX)]
        vT = [self.sb(es, [128, T], BF16, "vT") for _ in range(NX)]
        b_qkv = [Buf("qkv") for _ in range(NX)]
        Vtm = self.sb(es, [128, NT, 128], BF16, "Vtm")
        b_V = Buf("Vtm")
        OT = [self.sb(es, [128, T], BF16, "OT") for _ in range(NX)]
        b_OT = [Buf("OT") for _ in range(NX)]
        NR = 2
        A1 = [self.sb(es, [128, T], F32, "A1") for _ in range(NR)]
        A2 = [self.sb(es, [128, T], F32, "A2") for _ in range(NR)]
        A3 = [self.sb(es, [128, T], F32, "A3") for _ in range(NR)]
        A4 = [self.sb(es, [128, T], F32, "A4") for _ in range(NR)]
        Wb = [self.sb(es, [128, T], BF16, "Wb") for _ in range(NR)]
        WT = [self.sb(es, [128, NT, 128], BF16, "WT") for _ in range(NR)]
        sm = [self.sb(es, [128, 32], F32, "sm") for _ in range(NR)]
        dg = [self.sb(es, [128, 128], F32, "dg") for _ in range(NR)]
        onb = [self.sb(es, [128, 128], BF16, "onb") for _ in range(NR)]
        b_A = [Buf("A") for _ in range(NR)]
        b_Wb = [Buf("Wb") for _ in range(NR)]
        b_WT = [Buf("WT") for _ in range(NR)]
        b_on = [Buf("on") for _ in range(NR)]
        psS = [self.ps(es, [128, 512], F32, "psS") for _ in range(4)]
        b_pS = [Buf("pS") for _ in range(4)]
        psT = [self.ps(es, [128, 8, 128], BF16, "psT") for _ in range(2)]
        b_pT = [Buf("pT") for _ in range(2)]
        psO = [self.ps(es, [128, 128], F32, "psO") for _ in range(2)]
        b_pO = [Buf("pO") for _ in range(2)]
        cnt = {"S": 0, "T": 0, "O": 0, "row": 0}
        nq = c.NCL - 3 * c.CCH

        def load_head(i, qc, kc, vc):
            s.dma(s.q_sync, qT[i][:], self.qkvT[qc * 128:(qc + 1) * 128, :], writes=[b_qkv[i]])
            s.dma(s.q_sync, kT[i][:], self.qkvT[kc * 128:(kc + 1) * 128, :], writes=[b_qkv[i]])
            s.dma(s.q_sync, vT[i][:], self.qkvT[vc * 128:(vc + 1) * 128, :], writes=[b_qkv[i]])

        def build_V(i):
            for t0 in range(0, NT, 8):
                tn = min(8, NT - t0)
                P, bP = psT[cnt["T"] % 2], b_pT[cnt["T"] % 2]
                cnt["T"] += 1

                def tr(h):
                    last = None
                    for t in range(tn):
                        last = h.transpose(P[:, t, :], vT[i][:, (t0 + t) * 128:(t0 + t + 1) * 128], self.ident[:])
                    return last

                s.op(s.pe, tr, reads=[b_qkv[i], self.b_const], writes=[bP])
                s.op(s.dve, lambda h: h.tensor_copy(out=Vtm[:, t0:t0 + tn, :], in_=P[:, :tn, :]), reads=[bP], writes=[b_V])

        def transposes_W(r, nt):
            for t0 in range(0, nt, 8):
                tn = min(8, nt - t0)
                P, bP = psT[cnt["T"] % 2], b_pT[cnt["T"] % 2]
                cnt["T"] += 1

                def tr(h):
                    last = None
                    for t in range(tn):
                        last = h.transpose(P[:, t, :], Wb[r][:, (t0 + t) * 128:(t0 + t + 1) * 128], self.ident[:])
                    return last

                s.op(s.pe, tr, reads=[b_Wb[r], self.b_const], writes=[bP])
                s.op(s.act, lambda h: h.copy(out=WT[r][:, t0:t0 + tn, :], in_=P[:, :tn, :]), reads=[bP], writes=[b_WT[r]])

        def kblocks(nk):
            return [(o, min(512, nk - o)) for o in range(0, nk, 512)]

        hi = 0
        scale_d = 64 ** -0.5
        for hd in range(c.HD):
            i = hi % NX
            hi += 1
            load_head(i, 0 * c.HD + hd, 1 * c.HD + hd, 2 * c.HD + hd)
            build_V(i)
            for qi in range(NT):
                r = cnt["row"] % NR
                cnt["row"] += 1
                nk = (qi + 1) * 128
                blks = kblocks(nk)
                s.op(s.dve, lambda h: h.memset(sm[r][:], 0.0), writes=[b_A[r]])
                for cm in range(2):
                    PA = (A1 if cm == 0 else A2)[r]
                    for bi, (o, w) in enumerate(blks):
                        P, bP = psS[cnt["S"] % 4], b_pS[cnt["S"] % 4]
                        cnt["S"] += 1
                        s.op(s.pe, lambda h: h.matmul(P[:, :w], lhsT=qT[i][cm * 64:(cm + 1) * 64, qi * 128:(qi + 1) * 128],
                                                      rhs=kT[i][cm * 64:(cm + 1) * 64, o:o + w], start=True, stop=True),
                             reads=[b_qkv[i]], writes=[bP])
                        last = (bi == len(blks) - 1)
                        wn = w - 128 if last else w
                        col = cm * 8 + bi

                        def ex(h):
                            ins = None
                            if wn > 0:
                                ins = h.activation(out=PA[:, o:o + wn], in_=P[:, :wn], func=AF.Exp, scale=scale_d,
                                                   accum_out=sm[r][:, col:col + 1])
                            if last:
                                ins = h.activation(out=dg[r][:], in_=P[:, wn:w], func=AF.Exp, scale=scale_d)
                            return ins

                        s.op(s.act, ex, reads=[bP], writes=[b_A[r]])
                        if last:
                            s.chain(s.dve, [
                                lambda h: h.tensor_tensor(out=PA[:, nk - 128:nk], in0=dg[r][:], in1=self.mask_le[:], op=ALU.mult),
                                lambda h: h.reduce_sum(out=sm[r][:, cm * 8 + 7:cm * 8 + 8], in_=PA[:, nk - 128:nk], axis=AX.X),
                            ], reads=[self.b_const], writes=[b_A[r]])

                def comb_a(h):
                    h.reduce_sum(out=sm[r][:, 16:17], in_=sm[r][:, 0:8], axis=AX.X)
                    return h.reduce_sum(out=sm[r][:, 17:18], in_=sm[r][:, 8:16], axis=AX.X)

                def comb_c(h):
                    h.tensor_tensor(out=sm[r][:, 20:21], in0=sm[r][:, 19:20], in1=neglam, op=ALU.mult)
                    return h.tensor_scalar(out=A1[r][:, :nk], in0=A1[r][:, :nk], scalar1=sm[r][:, 18:19], scalar2=None,
                                           op0=ALU.mult)

                s.chain(s.dve, [
                    comb_a,
                    lambda h: h.reciprocal(out=sm[r][:, 18:20], in_=sm[r][:, 16:18]),
                    comb_c,
                    lambda h: h.scalar_tensor_tensor(out=Wb[r][:, :nk], in0=A2[r][:, :nk], scalar=sm[r][:, 20:21],
                                                     in1=A1[r][:, :nk], op0=ALU.mult, op1=ALU.add),
                ], reads=[b_par], writes=[b_A[r], b_Wb[r]])
                transposes_W(r, qi + 1)
                PO, bPO = psO[cnt["O"] % 2], b_pO[cnt["O"] % 2]
                cnt["O"] += 1

                def pv(h):
                    last = None
                    for kt in range(qi + 1):
                        last = h.matmul(PO[:, :], lhsT=WT[r][:, kt, :], rhs=Vtm[:, kt, :], start=(kt == 0), stop=(kt == qi))
                    return last

                s.op(s.pe, pv, reads=[b_WT[r], b_V], writes=[bPO])
                s.op(s.dve, lambda h: h.memset(sm[r][:, 24:25], 0.0), writes=[b_on[r]])
                s.op(s.act, lambda h: h.activation(out=dg[r][:], in_=PO[:, :], func=AF.Square, accum_out=sm[r][:, 24:25]),
                     reads=[bPO], writes=[b_on[r], b_A[r]])

                s.op(s.dve, lambda h: h.tensor_scalar(out=sm[r][:, 25:26], in0=sm[r][:, 24:25], scalar1=1.0 / 128,
                                                      scalar2=LN_EPS, op0=ALU.mult, op1=ALU.add), writes=[b_on[r]])

                s.chain(s.act, [
                    lambda h: h.activation(out=sm[r][:, 26:27], in_=sm[r][:, 25:26], func=AF.Ln),
                    lambda h: h.activation(out=sm[r][:, 26:27], in_=sm[r][:, 26:27], func=AF.Exp, scale=-0.5),
                ], writes=[b_on[r]])
                s.op(s.dve, lambda h: h.scalar_tensor_tensor(out=onb[r][:], in0=PO[:, :], scalar=sm[r][:, 26:27], in1=gt[:],
                                                             op0=ALU.mult, op1=ALU.mult),
                     reads=[bPO, b_par], writes=[b_on[r]])
                P, bP = psT[cnt["T"] % 2], b_pT[cnt["T"] % 2]
                cnt["T"] += 1
                s.op(s.pe, lambda h: h.transpose(P[:, 0, :], onb[r][:], self.ident[:]), reads=[b_on[r], self.b_const],
                     writes=[bP])
                s.op(s.act, lambda h: h.copy(out=OT[i][:, qi * 128:(qi + 1) * 128], in_=P[:, 0, :]), reads=[bP],
                     writes=[b_OT[i]])
            self.mix_store(OT[i], b_OT[i], c.CCH + hd)

        scale_s = 128 ** -0.5
        for hs in range(c.HS):
            i = hi % NX
            hi += 1
            base = 3 * c.HD
            load_head(i, base + 0 * c.HS + hs, base + 1 * c.HS + hs, base + 2 * c.HS + hs)
            build_V(i)
            for qi in range(NT):
                r = cnt["row"] % NR
                cnt["row"] += 1
                nk = (qi + 1) * 128
                blks = kblocks(nk)
                E, SP, ZS, CS = A1[r], A2[r], A3[r], A4[r]
                for bi, (o, w) in enumerate(blks):
                    P, bP = psS[cnt["S"] % 4], b_pS[cnt["S"] % 4]
                    cnt["S"] += 1
                    s.op(s.pe, lambda h: h.matmul(P[:, :w], lhsT=qT[i][:, qi * 128:(qi + 1) * 128], rhs=kT[i][:, o:o + w],
                                                  start=True, stop=True), reads=[b_qkv[i]], writes=[bP])
                    s.op(s.act, lambda h: h.activation(out=E[:, o:o + w], in_=P[:, :w], func=AF.Exp, scale=scale_s),
                         reads=[bP], writes=[b_A[r]])
                    s.op(s.dve, lambda h: h.tensor_scalar(out=ZS[:, o:o + w], in0=P[:, :w], scalar1=scale_s, scalar2=None,
                                                          op0=ALU.mult), reads=[bP], writes=[b_A[r]])
                s.op(s.act, lambda h: h.activation(out=SP[:, :nk], in_=E[:, :nk], func=AF.Ln, bias=1.0), writes=[b_A[r]])

                def scan_b(h):
                    h.tensor_tensor_scan(out=CS[:, :nk], data0=self.ones[:, :nk], data1=SP[:, :nk], initial=0.0,
                                         op0=ALU.mult, op1=ALU.add)
                    return h.tensor_tensor(out=ZS[:, :nk], in0=ZS[:, :nk], in1=SP[:, :nk], op=ALU.subtract)

                def scan_c(h):
                    h.tensor_tensor(out=ZS[:, :nk], in0=ZS[:, :nk], in1=CS[:, :nk], op=ALU.add)
                    return h.tensor_scalar(out=sm[r][:, 0:1], in0=CS[:, nk - 1:nk], scalar1=-1.0, scalar2=None, op0=ALU.mult)

                s.chain(s.dve, [
                    lambda h: h.tensor_tensor(out=SP[:, nk - 128:nk], in0=SP[:, nk - 128:nk], in1=self.mask_lt[:], op=ALU.mult),
                    scan_b, scan_c,
                ], reads=[self.b_const], writes=[b_A[r]])

                def wexp(h):
                    ins = None
                    if nk > 128:
                        ins = h.activation(out=Wb[r][:, :nk - 128], in_=ZS[:, :nk - 128], func=AF.Exp, bias=sm[r][:, 0:1])
                    return h.activation(out=dg[r][:], in_=ZS[:, nk - 128:nk], func=AF.Exp, bias=sm[r][:, 0:1])

                s.op(s.act, wexp, writes=[b_A[r], b_Wb[r]])
                s.op(s.dve, lambda h: h.tensor_tensor(out=Wb[r][:, nk - 128:nk], in0=dg[r][:], in1=self.mask_lt[:], op=ALU.mult),
                     reads=[self.b_const], writes=[b_A[r], b_Wb[r]])
                transposes_W(r, qi + 1)
                PO, bPO = psO[cnt["O"] % 2], b_pO[cnt["O"] % 2]
                cnt["O"] += 1

                def pv(h):
                    last = None
                    for kt in range(qi + 1):
                        last = h.matmul(PO[:, :], lhsT=Vtm[:, kt, :], rhs=WT[r][:, kt, :], start=(kt == 0), stop=(kt == qi))
                    return last

                s.op(s.pe, pv, reads=[b_WT[r], b_V], writes=[bPO])
                s.op(s.act, lambda h: h.copy(out=OT[i][:, qi * 128:(qi + 1) * 128], in_=PO[:, :]), reads=[bPO],
                     writes=[b_OT[i]])
            self.mix_store(OT[i], b_OT[i], c.CCH + c.HD + hs)

    def gemm_T(self, es_outer, XT, b_XT, nk, Wd, row0, bl, res_src, first):
        s, c = self.s, self.c
        D, TB = c.D, c.TB
        with ExitStack() as es:
            wb = [self.sb(es, [128, c.KC, 512], BF16, "wT") for _ in range(2)]
            b_w = [Buf("w") for _ in range(2)]
            pb = [self.ps(es, [128, 512], F32, "pT") for _ in range(6)]
            b_p = [Buf("p") for _ in range(6)]
            NS = 4
            rb = [self.sb(es, [128, 512], F32, "rT") for _ in range(NS)]
            yb = [self.sb(es, [128, 512], F32, "yT") for _ in range(NS)]
            b_r = [Buf("r") for _ in range(NS)]
            b_y = [Buf("y") for _ in range(NS)]
            b_dr = {}
            pi = 0
            si = 0
            for nb in range(D // 512):
                W, bW = wb[nb % 2], b_w[nb % 2]
                self.load_w(W, bW, Wd, row0, nk, nb * 512, 512)
                for (off, n) in c.mtiles:
                    P, bP = pb[pi % 6], b_p[pi % 6]
                    pi += 1
                    RB, YB, bR, bY = rb[si % NS], yb[si % NS], b_r[si % NS], b_y[si % NS]
                    si += 1
                    r0 = bl * TB + off
                    s.dma(s.q_sync, RB[:n, :], res_src[r0:r0 + n, nb * 512:(nb + 1) * 512], writes=[bR])

                    def mm(h):
                        last = None
                        for k in range(nk):
                            last = h.matmul(P[:n, :], lhsT=XT[:, k, off:off + n], rhs=W[:, k, :], start=(k == 0),
                                            stop=(k == nk - 1))
                        return last

                    s.op(s.pe, mm, reads=[bW, b_XT], writes=[bP])
                    sc = c.alpha if first else 1.0
                    s.op(s.dve, lambda h: h.scalar_tensor_tensor(out=YB[:n, :], in0=RB[:n, :], scalar=sc, in1=P[:n, :],
                                                                 op0=ALU.mult, op1=ALU.add),
                         reads=[bR, bP], writes=[bY])
                    s.dma(s.q_sync, self.ypre[r0:r0 + n, nb * 512:(nb + 1) * 512], YB[:n, :], reads=[bY])
            s.barrier()

    def stage_wup(self, l, XT, b_XT, XTh, b_XTh, blk):
        s, c = self.s, self.c
        D, KC, TB, DFF, FC = c.D, c.KC, c.TB, c.DFF, c.FC
        with ExitStack() as es:
            fw = self.sb(es, [128, 2 * FC * 3], F32, "fcw")
            b_fw = Buf("fcw")
            s.dma(s.q_sync, fw[:], self.fcw[l * 128:(l + 1) * 128, :], writes=[b_fw])
            wg = [self.sb(es, [128, KC, 256], BF16, "wg") for _ in range(2)]
            wu = [self.sb(es, [128, KC, 256], BF16, "wu") for _ in range(2)]
            b_w = [Buf("w") for _ in range(2)]
            pb = [self.ps(es, [128, 512], F32, "pF") for _ in range(6)]
            b_p = [Buf("p") for _ in range(6)]
            ph = [self.ps(es, [128, 16], F32, "pH") for _ in range(2)]
            b_ph = [Buf("ph") for _ in range(2)]
            NS = 2
            ug = [self.sb(es, [128, TB + 2], F32, "ug") for _ in range(NS)]
            uu = [self.sb(es, [128, TB + 2], F32, "uu") for _ in range(NS)]
            cg = [self.sb(es, [128, TB], F32, "cg") for _ in range(NS)]
            cu = [self.sb(es, [128, TB], F32, "cu") for _ in range(NS)]
            ab = [self.sb(es, [128, TB], BF16, "ab") for _ in range(NS)]
            b_ug = [Buf("ug") for _ in range(NS)]
            b_uu = [Buf("uu") for _ in range(NS)]
            b_cg = [Buf("cg") for _ in range(NS)]
            b_cu = [Buf("cu") for _ in range(NS)]
            b_ab = [Buf("ab") for _ in range(NS)]
            pi = 0
            hi = 0
            for g in range(FC // 2):
                WG, WU, bW = wg[g % 2], wu[g % 2], b_w[g % 2]
                self.load_w(WG, bW, self.w_up, l * D, KC, g * 256, 256)
                self.load_w(WU, bW, self.w_up, l * D, KC, DFF + g * 256, 256)
                for j in range(2):
                    ch = g * 2 + j
                    i = ch % NS
                    for (W, U, bU) in ((WG, ug[i], b_ug[i]), (WU, uu[i], b_uu[i])):
                        if blk == 0:
                            s.op(s.dve, lambda h: h.memset(U[:, 0:2], 0.0), writes=[bU])
                        else:
                            PH, bPH = ph[hi % 2], b_ph[hi % 2]
                            hi += 1

                            def mmh(h):
                                last = None
                                for k in range(KC):
                                    last = h.matmul(PH[:, 0:2], lhsT=W[:, k, j * 128:(j + 1) * 128], rhs=XTh[:, k, 0:2],
                                                    start=(k == 0), stop=(k == KC - 1))
                                return last

                            s.op(s.pe, mmh, reads=[bW, b_XTh], writes=[bPH])
                            s.op(s.act, lambda h: h.copy(out=U[:, 0:2], in_=PH[:, 0:2]), reads=[bPH], writes=[bU])
                        for (o, n) in c.tsubs:
                            P, bP = pb[pi % 6], b_p[pi % 6]
                            pi += 1

                            def mm(h):
                                last = None
                                for k in range(KC):
                                    last = h.matmul(P[:, :n], lhsT=W[:, k, j * 128:(j + 1) * 128], rhs=XT[:, k, o:o + n],
                                                    start=(k == 0), stop=(k == KC - 1))
                                return last

                            s.op(s.pe, mm, reads=[bW, b_XT], writes=[bP])
                            s.op(s.act, lambda h: h.copy(out=U[:, 2 + o:2 + o + n], in_=P[:, :n]), reads=[bP], writes=[bU])
                    UG, UU, CG, CU, AB = ug[i], uu[i], cg[i], cu[i], ab[i]
                    wcol = lambda cidx, k: fw[:, cidx * 3 + k:cidx * 3 + k + 1]

                    def cv0(h):
                        h.tensor_scalar(out=CG[:], in0=UG[:, 2:], scalar1=wcol(ch, 2), scalar2=None, op0=ALU.mult)
                        return h.tensor_scalar(out=CU[:], in0=UU[:, 2:], scalar1=wcol(FC + ch, 2), scalar2=None, op0=ALU.mult)

                    def cvk(k):
                        def f(h):
                            h.scalar_tensor_tensor(out=CG[:], in0=UG[:, k:TB + k], scalar=wcol(ch, k), in1=CG[:],
                                                   op0=ALU.mult, op1=ALU.add)
                            return h.scalar_tensor_tensor(out=CU[:], in0=UU[:, k:TB + k], scalar=wcol(FC + ch, k), in1=CU[:],
                                                          op0=ALU.mult, op1=ALU.add)
                        return f

                    s.chain(s.dve, [cv0, cvk(1), cvk(0)], reads=[b_ug[i], b_uu[i], b_fw], writes=[b_cg[i], b_cu[i]])
                    s.op(s.act, lambda h: h.activation(out=CG[:], in_=CG[:], func=AF.Silu), writes=[b_cg[i]])
                    s.op(s.pool, lambda h: h.tensor_tensor(out=AB[:], in0=CG[:], in1=CU[:], op=ALU.mult),
                         reads=[b_cg[i], b_cu[i]], writes=[b_ab[i]])
                    s.dma(s.q_sync, self.aT[ch * 128:(ch + 1) * 128, :], AB[:], reads=[b_ab[i]])
            s.barrier()

    def stage_dense(self, l):
        s, c = self.s, self.c
        D, KC, TB = c.D, c.KC, c.TB
        last_layer = (l == c.DEPTH - 1)
        with ExitStack() as es:
            XT = self.sb(es, [128, KC, TB], BF16, "XTd")
            b_XT = Buf("XT")
            XTh = self.sb(es, [128, KC, 2], BF16, "XTh")
            b_XTh = Buf("XTh")
            for bl in range(c.NBL):
                blk = bl
                s.dma(s.q_sync, XT[:, :, :], self.mixT[blk * c.MCL * 128:(blk + 1) * c.MCL * 128, :]
                      .rearrange("(k p) t -> p k t", p=128), writes=[b_XT])
                self.gemm_T(es, XT, b_XT, KC, self.w_out, l * D, bl, self.hres, True)
                self.ln_pass(es, self.ypre, self.ln1g[l:l + 1, :], self.ln1b[l:l + 1, :], XT, b_XT, bl, h_dst=self.hres)
                self.stage_wup(l, XT, b_XT, XTh, b_XTh, blk)
                if bl + 1 < c.NBL:
                    s.op(s.dve, lambda h: h.tensor_copy(out=XTh[:, :, :], in_=XT[:, :, TB - 2:TB]), reads=[b_XT],
                         writes=[b_XTh])
                    s.barrier()
                for pi, (f0, fn) in enumerate(c.fparts):
                    s.dma(s.q_sync, XT[:, :fn, :], self.aT[f0 * 128:(f0 + fn) * 128, :].rearrange("(k p) t -> p k t", p=128),
                          writes=[b_XT])
                    self.gemm_T(es, XT, b_XT, fn, self.w_down, l * c.DFF + f0 * 128, bl,
                                self.hres if pi == 0 else self.ypre, pi == 0)
                if last_layer:
                    self.ln_pass(es, self.ypre, self.ln2g[l:l + 1, :], self.ln2b[l:l + 1, :], None, None, bl, h_dst=self.out,
                                 dst_off=self.sq * c.NBL * TB)
                else:
                    dst = self.hnT[blk * KC * 128:(blk + 1) * KC * 128, :]
                    self.ln_pass(es, self.ypre, self.ln2g[l:l + 1, :], self.ln2b[l:l + 1, :], XT, b_XT, bl,
                                 h_dst=self.hres, hnT_dst=dst)


def make_in_maps(cfg: Cfg, inp):
    c = cfg
    f = lambda a: np.ascontiguousarray(np.asarray(a, dtype=np.float32))
    x = f(inp["x"])
    meta = f(inp["meta_tokens"])
    D, DEPTH = c.D, c.DEPTH
    rows = c.mix_rows()
    w_out = f(inp["w_out"])[:, rows, :].reshape(DEPTH * D, D)
    w_up = f(inp["w_up"]).reshape(DEPTH * D, 2 * c.DFF)
    w_down = f(inp["w_down"]).reshape(DEPTH * c.DFF, D)
    fcw = f(inp["ffn_conv_w"]).reshape(DEPTH, 3, 2 * c.FC, 128).transpose(0, 3, 2, 1).reshape(DEPTH * 128, 2 * c.FC * 3)
    lamv = np.stack([f(inp["lambda_q1"]), f(inp["lambda_k1"]), f(inp["lambda_q2"]), f(inp["lambda_k2"])], axis=1)
    lamv = np.ascontiguousarray(lamv.reshape(DEPTH * 4, 64))
    common = {
        "embg": f(inp["emb_ln_g"]).reshape(1, D), "embb": f(inp["emb_ln_b"]).reshape(1, D),
        "lamv": lamv, "dng": f(inp["diff_norm_g"]),
        "w_out": np.ascontiguousarray(w_out), "ln1g": f(inp["ln1_g"]), "ln1b": f(inp["ln1_b"]),
        "w_up": w_up, "fcw": np.ascontiguousarray(fcw), "w_down": w_down,
        "ln2g": f(inp["ln2_g"]), "ln2b": f(inp["ln2_b"]),
    }
    per_rank = []
    for r in range(c.NPAIR):
        cols = c.in_cols(r)
        w_in = np.ascontiguousarray(f(inp["w_in"])[:, :, cols].reshape(DEPTH * D, c.NCL * 128))
        scw = f(inp["short_conv_w"])[:, :, r * c.CCH * 128:(r + 1) * c.CCH * 128]
        scw = scw.reshape(DEPTH, 3, c.CCH, 128).transpose(0, 3, 2, 1).reshape(DEPTH * 128, c.CCH * 3)
        per_rank.append({"w_in": w_in, "scw": np.ascontiguousarray(scw)})
    maps = []
    for core in range(c.B // c.SPC):
        toks = np.zeros((c.SPC, c.T, D), np.float32)
        for j in range(c.SPC):
            toks[j, :N_META] = meta
            toks[j, N_META:c.L] = x[core * c.SPC + j]
        m = dict(common)
        m.update(per_rank[0])
        m["tok"] = toks.reshape(c.SPC * c.T, D)
        maps.append(m)
    return maps


def gather_out(cfg: Cfg, results):
    c = cfg
    out = np.zeros((c.B, c.SEQ, c.D), np.float32)
    for core in range(c.B // c.SPC):
        rows = np.asarray(results[core]["out"]).reshape(c.SPC, c.T, c.D)
        for j in range(c.SPC):
            out[core * c.SPC + j] = rows[j, N_META:c.L]
    return out


_CACHE = {}


def run(cfg: Cfg, inp):
    key = (cfg.D, cfg.SEQ, cfg.DEPTH, cfg.NPAIR, cfg.B, cfg.SPC)
    if key not in _CACHE:
        _CACHE[key] = Builder(cfg).build()
    nc = _CACHE[key]
    maps = make_in_maps(cfg, inp)
    res = run_bass_kernel_spmd(nc, maps, core_ids=list(range(len(maps))))
    return gather_out(cfg, res.results)


def kernel(**inputs):
    cfg = Cfg(D=4096, SEQ=2048, DEPTH=4, NPAIR=1, B=4, SPC=2)
    return run(cfg, inputs)
```

```python
import math
from contextlib import ExitStack

import numpy as np
import ml_dtypes
import concourse.bass as bass
import concourse.mybir as mybir
from concourse.bass_utils import run_bass_kernel_spmd

F32 = mybir.dt.float32
BF16 = mybir.dt.bfloat16
AF = mybir.ActivationFunctionType
ALU = mybir.AluOpType
AX = mybir.AxisListType

N_META = 16
LN_EPS = 1e-5
SAME_ENGINE_SYNC = True
DMA_POOL = 8


class Cfg:
    def __init__(self, D=4096, SEQ=2048, DEPTH=4, NPAIR=1, B=4, SPC=1):
        self.D, self.SEQ, self.DEPTH, self.NPAIR, self.B, self.SPC = D, SEQ, DEPTH, NPAIR, B, SPC
        assert NPAIR == 1 and B % SPC == 0
        self.L = N_META + SEQ
        self.NB = 2
        t = -(-self.L // 128) * 128
        if (t // 2) % 64:
            t += 128
        self.T = t
        self.TB = t // 2
        self.NBL = self.NB // NPAIR
        self.KC = D // 128
        self.CONV = D // 4
        self.NDH = (3 * D) // (8 * 128)
        self.NSH = self.NDH
        self.DFF = ((8 * D // 3 + 255) // 256) * 256
        self.FC = self.DFF // 128
        self.CCH = self.CONV // 128 // NPAIR
        self.HD = self.NDH // NPAIR
        self.HS = self.NSH // NPAIR
        self.NCL = 3 * self.CCH + 3 * self.HD + 3 * self.HS
        self.MCL = self.CCH + self.HD + self.HS
        self.alpha = (2 * DEPTH) ** 0.25
        assert self.MCL * NPAIR == self.KC
        self.mtiles = [(o, min(128, self.TB - o)) for o in range(0, self.TB, 128)]
        self.tsubs = [(o, min(512, self.TB - o)) for o in range(0, self.TB, 512)]
        nparts = -(-self.FC // self.KC)
        base = self.FC // nparts
        rem = self.FC % nparts
        self.fparts = []
        o = 0
        for i in range(nparts):
            n = base + (1 if i < rem else 0)
            self.fparts.append((o, n))
            o += n

    def lam_init(self, layer):
        return 0.8 - 0.6 * math.exp(-0.3 * layer)

    def in_cols(self, r):
        C, DW = self.CONV, self.NDH * 128
        cl, dl, sl = self.CCH * 128, self.HD * 128, self.HS * 128
        cols = []
        for i in range(3):
            cols.append(np.arange(i * C + r * cl, i * C + (r + 1) * cl))
        for i in range(3):
            cols.append(np.arange(3 * C + i * DW + r * dl, 3 * C + i * DW + (r + 1) * dl))
        for i in range(3):
            cols.append(np.arange(3 * C + 3 * DW + i * DW + r * sl, 3 * C + 3 * DW + i * DW + (r + 1) * sl))
        return np.concatenate(cols)

    def mix_rows(self):
        C, DW = self.CONV, self.NDH * 128
        cl, dl, sl = self.CCH * 128, self.HD * 128, self.HS * 128
        rows = []
        for r in range(self.NPAIR):
            rows.append(np.arange(r * cl, (r + 1) * cl))
            rows.append(np.arange(C + r * dl, C + (r + 1) * dl))
            rows.append(np.arange(C + DW + r * sl, C + DW + (r + 1) * sl))
        return np.concatenate(rows)


class Buf:
    __slots__ = ("name", "w", "r")

    def __init__(self, name):
        self.name = name
        self.w = None
        self.r = {}


class Eng:
    def __init__(self, name, h, sem, sync_self):
        self.name, self.h, self.sem, self.cnt = name, h, sem, 0
        self.waited = {}
        self.sync_self = sync_self


class DmaQ:
    def __init__(self, eng, sems):
        self.eng, self.sems, self.idx = eng, sems, 0


class Sched:
    def __init__(self, nc, es):
        self.nc = nc
        mk = lambda n: es.enter_context(nc.semaphore(n))
        self.pe = Eng("pe", nc.tensor, mk("s_pe"), False)
        self.act = Eng("act", nc.scalar, mk("s_act"), SAME_ENGINE_SYNC)
        self.dve = Eng("dve", nc.vector, mk("s_dve"), SAME_ENGINE_SYNC)
        self.pool = Eng("pool", nc.gpsimd, mk("s_pool"), SAME_ENGINE_SYNC)
        self.sp = Eng("sp", nc.sync, mk("s_sp"), False)
        self.engs = [self.pe, self.act, self.dve, self.pool, self.sp]
        self.q_sync = DmaQ(self.sp, [mk(f"s_dq{i}") for i in range(DMA_POOL)])
        self.q_pool = DmaQ(self.pool, [mk(f"s_dp{i}") for i in range(DMA_POOL)])
        self.queues = [self.q_sync, self.q_pool]
        self.n_inst = 0

    def _deps(self, reads, writes):
        d = {}

        def add(ev):
            if ev is not None:
                s, v = ev
                if d.get(s, (None, 0))[1] < v:
                    d[s] = (s, v)

        for b in reads:
            add(b.w)
        for b in writes:
            add(b.w)
            for s, v in b.r.items():
                add((s, v))
        return list(d.values())

    def _wait(self, eng, deps):
        for s, v in deps:
            if s is eng.sem and not eng.sync_self:
                continue
            if eng.waited.get(s, 0) < v:
                eng.h.wait_ge(s, v)
                eng.waited[s] = v
                self.n_inst += 1

    def _commit(self, ev, reads, writes):
        s, v = ev
        for b in writes:
            b.w = ev
            b.r = {}
        for b in reads:
            if b.r.get(s, 0) < v:
                b.r[s] = v

    def op(self, eng, emit, reads=(), writes=()):
        self._wait(eng, self._deps(reads, writes))
        inst = emit(eng.h)
        eng.cnt += 1
        inst.then_inc(eng.sem, 1)
        self.n_inst += 1
        self._commit((eng.sem, eng.cnt), reads, writes)

    def chain(self, eng, emits, reads=(), writes=()):
        for e in emits:
            self.op(eng, e, reads=reads, writes=writes)

    def dma(self, q, out, in_, reads=(), writes=()):
        slot, rnd = q.idx % len(q.sems), q.idx // len(q.sems)
        q.idx += 1
        sem = q.sems[slot]
        deps = self._deps(reads, writes)
        if rnd > 0:
            deps.append((sem, 16 * rnd))
        self._wait(q.eng, deps)
        q.eng.h.dma_start(out=out, in_=in_).then_inc(sem, 16)
        self.n_inst += 1
        self._commit((sem, 16 * (rnd + 1)), reads, writes)

    def all_events(self):
        evs = [(e.sem, e.cnt) for e in self.engs if e.cnt > 0]
        for q in self.queues:
            for i, s in enumerate(q.sems):
                n = (q.idx - i + len(q.sems) - 1) // len(q.sems)
                if n > 0:
                    evs.append((s, 16 * n))
        return evs

    def barrier(self, engs=None):
        evs = self.all_events()
        for e in (engs or self.engs):
            sv = e.sync_self
            e.sync_self = True
            self._wait(e, evs)
            e.sync_self = sv


class Builder:
    def __init__(self, cfg: Cfg, debug=False):
        self.debug = debug
        self.c = cfg
        self.nc = bass.Bass("TRN2", target_bir_lowering=False)
        self._uid = 0

    def uid(self, p):
        self._uid += 1
        return f"{p}{self._uid}"

    def sb(self, es, shape, dt, name=None):
        return es.enter_context(self.nc.sbuf_tensor(self.uid(name or "sb"), list(shape), dt))

    def ps(self, es, shape, dt, name=None):
        return es.enter_context(self.nc.psum_tensor(self.uid(name or "ps"), list(shape), dt))

    def dram(self, name, shape, dt, kind="Internal"):
        if kind == "Internal" and self.debug:
            kind = "ExternalOutput"
        return self.nc.dram_tensor(name, list(shape), dt, kind=kind).ap()

    def build(self):
        c, nc = self.c, self.nc
        D, T, TB, KC, DEPTH = c.D, c.T, c.TB, c.KC, c.DEPTH
        R = c.NBL * TB
        ein = lambda n, s: self.dram(n, s, F32, kind="ExternalInput")
        self.tok = ein("tok", [c.SPC * R, D])
        self.embg = ein("embg", [1, D])
        self.embb = ein("embb", [1, D])
        self.w_in = ein("w_in", [DEPTH * D, c.NCL * 128])
        self.scw = ein("scw", [DEPTH * 128, c.CCH * 3])
        self.lamv = ein("lamv", [DEPTH * 4, 64])
        self.dng = ein("dng", [DEPTH, 128])
        self.w_out = ein("w_out", [DEPTH * D, D])
        self.ln1g = ein("ln1g", [DEPTH, D])
        self.ln1b = ein("ln1b", [DEPTH, D])
        self.w_up = ein("w_up", [DEPTH * D, 2 * c.DFF])
        self.fcw = ein("fcw", [DEPTH * 128, 2 * c.FC * 3])
        self.w_down = ein("w_down", [DEPTH * c.DFF, D])
        self.ln2g = ein("ln2g", [DEPTH, D])
        self.ln2b = ein("ln2b", [DEPTH, D])
        self.out = self.dram("out", [c.SPC * R, D], F32, kind="ExternalOutput")
        self.hres = self.dram("hres", [R, D], F32)
        self.ypre = self.dram("ypre", [R, D], F32)
        self.hnT = self.dram("hnT", [c.NB * KC * 128, TB], BF16)
        self.qkvT = self.dram("qkvT", [(c.NCL - 3 * c.CCH) * 128, T], BF16)
        self.cvT = self.dram("cvT", [3 * c.CCH * 128, T], F32)
        self.mixT = self.dram("mixT", [c.NB * c.MCL * 128, TB], BF16)
        self.aT = self.dram("aT", [c.FC * 128, TB], BF16)

        with ExitStack() as es:
            self.s = Sched(nc, es)
            self.consts(es)
            self.s.barrier()
            for sq in range(c.SPC):
                self.sq = sq
                self.stage_embed()
                for l in range(DEPTH):
                    self.stage_win(l)
                    self.stage_mixers(l)
                    self.stage_dense(l)
            self.s.barrier()
        return nc

    def consts(self, es):
        s, c = self.s, self.c
        self.ident = self.sb(es, [128, 128], BF16, "ident")
        self.mask_le = self.sb(es, [128, 128], F32, "mle")
        self.mask_lt = self.sb(es, [128, 128], F32, "mlt")
        self.ones = self.sb(es, [128, c.T], F32, "ones")
        self.b_const = Buf("const")
        g = self.nc.gpsimd

        sel = lambda t, op: (lambda h: h.affine_select(out=t[:], in_=t[:], pattern=[[-1, 128]], compare_op=op,
                                                       fill=0.0, base=0, channel_multiplier=1))
        s.chain(s.pool, [
            lambda h: h.memset(self.ident[:], 1.0), sel(self.ident, ALU.is_equal),
            lambda h: h.memset(self.mask_le[:], 1.0), sel(self.mask_le, ALU.is_ge),
            lambda h: h.memset(self.mask_lt[:], 1.0), sel(self.mask_lt, ALU.is_gt),
            lambda h: h.memset(self.ones[:], 1.0),
        ], writes=[self.b_const])

    def ln_pass(self, es, src, g_ap, b_ap, XT, b_XT, blk_local, h_dst=None, hnT_dst=None, src_off=0, dst_off=0):
        s, c, nc = self.s, self.c, self.nc
        D, KC, TB = c.D, c.KC, c.TB
        with ExitStack() as st:
            gt = self.sb(st, [128, D], F32, "lng")
            bt = self.sb(st, [128, D], F32, "lnb")
            b_gb = Buf("gb")
            s.dma(s.q_sync, gt[:], g_ap.partition_broadcast(128), writes=[b_gb])
            s.dma(s.q_sync, bt[:], b_ap.partition_broadcast(128), writes=[b_gb])
            NX = 2
            xt = [self.sb(st, [128, D], F32, "lnx") for _ in range(NX)]
            hb = [self.sb(st, [128, D], BF16, "lnhb") for _ in range(NX)]
            st4 = [self.sb(st, [128, 8], F32, "lnst") for _ in range(NX)]
            junk = self.sb(st, [128, D], BF16, "lnjunk")
            b_x = [Buf("lnx") for _ in range(NX)]
            b_hb = [Buf("lnhb") for _ in range(NX)]
            b_st = [Buf("lnst") for _ in range(NX)]
            b_junk = Buf("junk")
            pt = [self.ps(st, [128, 8, 128], BF16, "lnpt") for _ in range(2)]
            b_pt = [Buf("lnpt") for _ in range(2)]
            pti = 0
            for ti, (off, n) in enumerate(c.mtiles):
                i = ti % NX
                r0 = blk_local * TB + off
                X, HB, ST = xt[i], hb[i], st4[i]
                s.dma(s.q_sync, X[:n, :], src[src_off + r0:src_off + r0 + n, :], writes=[b_x[i]])
                s.op(s.dve, lambda h: h.memset(ST[:n, :], 0.0), writes=[b_st[i]])
                s.op(s.act, lambda h: h.activation(out=junk[:n, :], in_=X[:n, :], func=AF.Identity,
                                                   accum_out=ST[:n, 0:1]),
                     reads=[b_x[i]], writes=[b_junk, b_st[i]])
                s.op(s.act, lambda h: h.activation(out=junk[:n, :], in_=X[:n, :], func=AF.Square,
                                                   accum_out=ST[:n, 1:2]),
                     reads=[b_x[i]], writes=[b_junk, b_st[i]])

                s.chain(s.dve, [
                    lambda h: h.tensor_scalar(out=ST[:n, 2:3], in0=ST[:n, 0:1], scalar1=1.0 / D, scalar2=None, op0=ALU.mult),
                    lambda h: h.tensor_tensor(out=ST[:n, 4:5], in0=ST[:n, 2:3], in1=ST[:n, 2:3], op=ALU.mult),
                    lambda h: h.scalar_tensor_tensor(out=ST[:n, 5:6], in0=ST[:n, 1:2], scalar=1.0 / D, in1=ST[:n, 4:5],
                                                     op0=ALU.mult, op1=ALU.subtract),
                    lambda h: h.tensor_scalar(out=ST[:n, 5:6], in0=ST[:n, 5:6], scalar1=LN_EPS, scalar2=None, op0=ALU.add),
                ], writes=[b_st[i]])
                s.chain(s.act, [
                    lambda h: h.activation(out=ST[:n, 6:7], in_=ST[:n, 5:6], func=AF.Ln),
                    lambda h: h.activation(out=ST[:n, 6:7], in_=ST[:n, 6:7], func=AF.Exp, scale=-0.5),
                ], writes=[b_st[i]])
                s.op(s.dve, lambda h: h.scalar_tensor_tensor(out=ST[:n, 7:8], in0=ST[:n, 2:3], scalar=-1.0,
                                                             in1=ST[:n, 6:7], op0=ALU.mult, op1=ALU.mult),
                     writes=[b_st[i]])
                s.op(s.act, lambda h: h.activation(out=X[:n, :], in_=X[:n, :], func=AF.Identity,
                                                   scale=ST[:n, 6:7], bias=ST[:n, 7:8]),
                     reads=[b_st[i]], writes=[b_x[i]])
                s.op(s.dve, lambda h: h.tensor_tensor(out=X[:n, :], in0=X[:n, :], in1=gt[:n, :], op=ALU.mult),
                     reads=[b_gb], writes=[b_x[i]])
                s.op(s.dve, lambda h: h.tensor_tensor(out=X[:n, :], in0=X[:n, :], in1=bt[:n, :], op=ALU.add),
                     reads=[b_gb], writes=[b_x[i]])
                if h_dst is not None:
                    s.dma(s.q_sync, h_dst[dst_off + r0:dst_off + r0 + n, :], X[:n, :], reads=[b_x[i]])
                if XT is not None:
                    s.op(s.pool, lambda h: h.tensor_copy(out=HB[:n, :], in_=X[:n, :]), reads=[b_x[i]], writes=[b_hb[i]])
                    for k0 in range(0, KC, 8):
                        kn = min(8, KC - k0)
                        P, bP = pt[pti % 2], b_pt[pti % 2]
                        pti += 1

                        def tr(h):
                            last = None
                            for k in range(kn):
                                last = h.transpose(P[:, k, :n], HB[:n, (k0 + k) * 128:(k0 + k + 1) * 128],
                                                   self.ident[:n, :n])
                            return last

                        s.op(s.pe, tr, reads=[b_hb[i], self.b_const], writes=[bP])
                        eng = s.act if (k0 // 8) % 2 == 0 else s.dve
                        if eng is s.act:
                            s.op(eng, lambda h: h.copy(out=XT[:, k0:k0 + kn, off:off + n], in_=P[:, :kn, :n]),
                                 reads=[bP], writes=[b_XT])
                        else:
                            s.op(eng, lambda h: h.tensor_copy(out=XT[:, k0:k0 + kn, off:off + n], in_=P[:, :kn, :n]),
                                 reads=[bP], writes=[b_XT])
            if hnT_dst is not None:
                s.dma(s.q_sync, hnT_dst.rearrange("(k p) t -> p k t", p=128), XT[:, :KC, :TB], reads=[b_XT])
            s.barrier()

    def stage_embed(self):
        s, c = self.s, self.c
        with ExitStack() as es:
            XT = self.sb(es, [128, c.KC, c.TB], BF16, "XTe")
            b_XT = Buf("XT")
            for bl in range(c.NBL):
                blk = bl
                dst = self.hnT[blk * c.KC * 128:(blk + 1) * c.KC * 128, :]
                self.ln_pass(es, self.tok, self.embg[0:1, :], self.embb[0:1, :], XT, b_XT, bl,
                             h_dst=self.hres, hnT_dst=dst, src_off=self.sq * c.NBL * c.TB)

    def load_w(self, wbuf3, bW, Wd, row0, nk, col0, ncols):
        s = self.s
        for k0 in range(0, nk, 8):
            kn = min(8, nk - k0)
            src = Wd[row0 + k0 * 128: row0 + (k0 + kn) * 128, col0:col0 + ncols].rearrange("(k p) c -> p k c", p=128)
            s.dma(s.q_pool, wbuf3[:, k0:k0 + kn, :ncols], src, writes=[bW])

    def stage_win(self, l):
        s, c = self.s, self.c
        D, KC, TB, T = c.D, c.KC, c.TB, c.T
        NG = -(-c.NCL // 4)
        with ExitStack() as es:
            XT = self.sb(es, [128, KC, TB], BF16, "XT1")
            b_XT = Buf("XT")
            wb = [self.sb(es, [128, KC, 512], BF16, "w1") for _ in range(2)]
            b_w = [Buf("w") for _ in range(2)]
            pb = [self.ps(es, [128, 512], F32, "p1") for _ in range(6)]
            b_p = [Buf("p") for _ in range(6)]
            NS = 4
            stb = [self.sb(es, [128, TB], BF16, "st1b") for _ in range(NS)]
            stf = [self.sb(es, [128, TB], F32, "st1f") for _ in range(NS)]
            b_st = [Buf("st") for _ in range(NS)]
            pi = 0
            si = 0
            gi = 0
            for blk in range(c.NB):
                s.dma(s.q_sync, XT[:, :, :], self.hnT[blk * KC * 128:(blk + 1) * KC * 128, :]
                      .rearrange("(k p) t -> p k t", p=128), writes=[b_XT])
                for g in range(NG):
                    W, bW = wb[gi % 2], b_w[gi % 2]
                    gi += 1
                    ncols = min(512, c.NCL * 128 - g * 512)
                    self.load_w(W, bW, self.w_in, l * D, KC, g * 512, ncols)
                    for j in range(ncols // 128):
                        ch = g * 4 + j
                        isconv = ch < 3 * c.CCH
                        ST = (stf if isconv else stb)[si % NS]
                        bS = b_st[si % NS]
                        si += 1
                        for (o, n) in c.tsubs:
                            P, bP = pb[pi % 6], b_p[pi % 6]
                            pi += 1

                            def mm(h):
                                last = None
                                for k in range(KC):
                                    last = h.matmul(P[:, :n], lhsT=W[:, k, j * 128:(j + 1) * 128], rhs=XT[:, k, o:o + n],
                                                    start=(k == 0), stop=(k == KC - 1))
                                return last

                            s.op(s.pe, mm, reads=[bW, b_XT], writes=[bP])
                            s.op(s.act, lambda h: h.copy(out=ST[:, o:o + n], in_=P[:, :n]), reads=[bP], writes=[bS])
                        if isconv:
                            dst = self.cvT[ch * 128:(ch + 1) * 128, blk * TB:(blk + 1) * TB]
                        else:
                            q = ch - 3 * c.CCH
                            dst = self.qkvT[q * 128:(q + 1) * 128, blk * TB:(blk + 1) * TB]
                        s.dma(s.q_sync, dst, ST[:, :], reads=[bS])
            s.barrier()

    def mix_store(self, OT, bO, chunk):
        s, c = self.s, self.c
        for blk in range(c.NB):
            r = (blk * c.MCL + chunk) * 128
            s.dma(s.q_sync, self.mixT[r:r + 128, :], OT[:, blk * c.TB:(blk + 1) * c.TB], reads=[bO])

    def stage_mixers(self, l):
        s, c, nc = self.s, self.c, self.nc
        T = c.T
        NT = T // 128
        with ExitStack() as es:
            wt = self.sb(es, [128, c.CCH * 3], F32, "scw")
            b_wt = Buf("scw")
            s.dma(s.q_sync, wt[:], self.scw[l * 128:(l + 1) * 128, :], writes=[b_wt])
            NX = 2
            cb = [self.sb(es, [128, T], F32, "cb") for _ in range(NX)]
            cc = [self.sb(es, [128, T], F32, "cc") for _ in range(NX)]
            zb = [self.sb(es, [128, T + 2], F32, "zb") for _ in range(NX)]
            yb = [self.sb(es, [128, T], F32, "yb") for _ in range(NX)]
            ob = [self.sb(es, [128, T], BF16, "ob") for _ in range(NX)]
            b_cb = [Buf("cb") for _ in range(NX)]
            b_cc = [Buf("cc") for _ in range(NX)]
            b_zb = [Buf("zb") for _ in range(NX)]
            b_yb = [Buf("yb") for _ in range(NX)]
            b_ob = [Buf("ob") for _ in range(NX)]
            for ch in range(c.CCH):
                i = ch % NX
                CB, CC, ZB, YB, OB = cb[i], cc[i], zb[i], yb[i], ob[i]
                s.dma(s.q_sync, CB[:], self.cvT[ch * 128:(ch + 1) * 128, :], writes=[b_cb[i]])
                s.dma(s.q_sync, CC[:], self.cvT[(c.CCH + ch) * 128:(c.CCH + ch + 1) * 128, :], writes=[b_cc[i]])
                s.dma(s.q_sync, ZB[:, 2:], self.cvT[(2 * c.CCH + ch) * 128:(2 * c.CCH + ch + 1) * 128, :],
                      writes=[b_zb[i]])

                s.chain(s.dve, [
                    lambda h: h.memset(ZB[:, 0:2], 0.0),
                    lambda h: h.tensor_tensor(out=ZB[:, 2:], in0=ZB[:, 2:], in1=CC[:], op=ALU.mult),
                    lambda h: h.tensor_scalar(out=YB[:], in0=ZB[:, 2:], scalar1=wt[:, ch * 3 + 2:ch * 3 + 3], scalar2=None,
                                              op0=ALU.mult),
                    lambda h: h.scalar_tensor_tensor(out=YB[:], in0=ZB[:, 1:T + 1], scalar=wt[:, ch * 3 + 1:ch * 3 + 2],
                                                     in1=YB[:], op0=ALU.mult, op1=ALU.add),
                    lambda h: h.scalar_tensor_tensor(out=YB[:], in0=ZB[:, 0:T], scalar=wt[:, ch * 3:ch * 3 + 1],
                                                     in1=YB[:], op0=ALU.mult, op1=ALU.add),
                    lambda h: h.tensor_tensor(out=OB[:], in0=YB[:], in1=CB[:], op=ALU.mult),
                ], reads=[b_cb[i], b_cc[i], b_wt], writes=[b_zb[i], b_yb[i], b_ob[i]])
                self.mix_store(OB, b_ob[i], ch)
            s.barrier()
        with ExitStack() as es:
            self.attn_heads(es, l)
            s.barrier()

    def attn_heads(self, es, l):
        s, c, nc = self.s, self.c, self.nc
        T = c.T
        NT = T // 128
        lam_init = c.lam_init(l)
        lq = self.sb(es, [128, 4, 64], F32, "lq")
        lam = self.sb(es, [128, 8], F32, "lam")
        gt = self.sb(es, [128, 128], F32, "dng")
        b_par = Buf("par")
        for i in range(4):
            s.dma(s.q_sync, lq[:, i, :], self.lamv[l * 4 + i:l * 4 + i + 1, :].partition_broadcast(128), writes=[b_par])
        s.dma(s.q_sync, gt[:], self.dng[l:l + 1, :].partition_broadcast(128), writes=[b_par])

        def lam_mul(h):
            h.tensor_tensor(out=lq[:, 0, :], in0=lq[:, 0, :], in1=lq[:, 1, :], op=ALU.mult)
            return h.tensor_tensor(out=lq[:, 2, :], in0=lq[:, 2, :], in1=lq[:, 3, :], op=ALU.mult)

        def lam_red(h):
            h.reduce_sum(out=lam[:, 0:1], in_=lq[:, 0, :], axis=AX.X)
            return h.reduce_sum(out=lam[:, 1:2], in_=lq[:, 2, :], axis=AX.X)

        s.chain(s.dve, [lam_mul, lam_red], writes=[b_par])
        s.op(s.act, lambda h: h.activation(out=lam[:, 2:4], in_=lam[:, 0:2], func=AF.Exp), writes=[b_par])
        s.chain(s.dve, [
            lambda h: h.tensor_tensor(out=lam[:, 4:5], in0=lam[:, 2:3], in1=lam[:, 3:4], op=ALU.subtract),
            lambda h: h.tensor_scalar(out=lam[:, 5:6], in0=lam[:, 4:5], scalar1=lam_init, scalar2=-1.0, op0=ALU.add,
                                      op1=ALU.mult),
            lambda h: h.tensor_scalar(out=gt[:], in0=gt[:], scalar1=1.0 - lam_init, scalar2=None, op0=ALU.mult),
        ], writes=[b_par])
        neglam = lam[:, 5:6]

        NX = 2
        qT = [self.sb(es, [128, T], BF16, "qT") for _ in range(NX)]
        kT = [self.sb(es, [128, T], BF16, "kT") for _ in range(NX)]
        vT = [self.sb(es, [128, T], BF16, "vT") for _ in range(NX)]
        b_qkv = [Buf("qkv") for _ in range(NX)]
        Vtm = self.sb(es, [128, NT, 128], BF16, "Vtm")
        b_V = Buf("Vtm")
        OT = [self.sb(es, [128, T], BF16, "OT") for _ in range(NX)]
        b_OT = [Buf("OT") for _ in range(NX)]
        NR = 2
        A1 = [self.sb(es, [128, T], F32, "A1") for _ in range(NR)]
        A2 = [self.sb(es, [128, T], F32, "A2") for _ in range(NR)]
        A3 = [self.sb(es, [128, T], F32, "A3") for _ in range(NR)]
        A4 = [self.sb(es, [128, T], F32, "A4") for _ in range(NR)]
        Wb = [self.sb(es, [128, T], BF16, "Wb") for _ in range(NR)]
        WT = [self.sb(es, [128, NT, 128], BF16, "WT") for _ in range(NR)]
        sm = [self.sb(es, [128, 32], F32, "sm") for _ in range(NR)]
        dg = [self.sb(es, [128, 128], F32, "dg") for _ in range(NR)]
        onb = [self.sb(es, [128, 128], BF16, "onb") for _ in range(NR)]
        b_A = [Buf("A") for _ in range(NR)]
        b_Wb = [Buf("Wb") for _ in range(NR)]
        b_WT = [Buf("WT") for _ in range(NR)]
        b_on = [Buf("on") for _ in range(NR)]
        psS = [self.ps(es, [128, 512], F32, "psS") for _ in range(4)]
        b_pS = [Buf("pS") for _ in range(4)]
        psT = [self.ps(es, [128, 8, 128], BF16, "psT") for _ in range(2)]
        b_pT = [Buf("pT") for _ in range(2)]
        psO = [self.ps(es, [128, 128], F32, "psO") for _ in range(2)]
        b_pO = [Buf("pO") for _ in range(2)]
        cnt = {"S": 0, "T": 0, "O": 0, "row": 0}
        nq = c.NCL - 3 * c.CCH

        def load_head(i, qc, kc, vc):
            s.dma(s.q_sync, qT[i][:], self.qkvT[qc * 128:(qc + 1) * 128, :], writes=[b_qkv[i]])
            s.dma(s.q_sync, kT[i][:], self.qkvT[kc * 128:(kc + 1) * 128, :], writes=[b_qkv[i]])
            s.dma(s.q_sync, vT[i][:], self.qkvT[vc * 128:(vc + 1) * 128, :], writes=[b_qkv[i]])

        def build_V(i):
            for t0 in range(0, NT, 8):
                tn = min(8, NT - t0)
                P, bP = psT[cnt["T"] % 2], b_pT[cnt["T"] % 2]
                cnt["T"] += 1

                def tr(h):
                    last = None
                    for t in range(tn):
                        last = h.transpose(P[:, t, :], vT[i][:, (t0 + t) * 128:(t0 + t + 1) * 128], self.ident[:])
                    return last

                s.op(s.pe, tr, reads=[b_qkv[i], self.b_const], writes=[bP])
                s.op(s.dve, lambda h: h.tensor_copy(out=Vtm[:, t0:t0 + tn, :], in_=P[:, :tn, :]), reads=[bP], writes=[b_V])

        def transposes_W(r, nt):
            for t0 in range(0, nt, 8):
                tn = min(8, nt - t0)
                P, bP = psT[cnt["T"] % 2], b_pT[cnt["T"] % 2]
                cnt["T"] += 1

                def tr(h):
                    last = None
                    for t in range(tn):
                        last = h.transpose(P[:, t, :], Wb[r][:, (t0 + t) * 128:(t0 + t + 1) * 128], self.ident[:])
                    return last

                s.op(s.pe, tr, reads=[b_Wb[r], self.b_const], writes=[bP])
                s.op(s.act, lambda h: h.copy(out=WT[r][:, t0:t0 + tn, :], in_=P[:, :tn, :]), reads=[bP], writes=[b_WT[r]])

        def kblocks(nk):
            return [(o, min(512, nk - o)) for o in range(0, nk, 512)]

        scale_d = 64 ** -0.5
        scale_s = 128 ** -0.5

        def d_front(i, qi):
            r = qi % NR
            nk = (qi + 1) * 128
            blks = kblocks(nk)
            s.op(s.dve, lambda h: h.memset(sm[r][:], 0.0), writes=[b_A[r]])
            for cm in range(2):
                PA = (A1 if cm == 0 else A2)[r]
                for bi, (o, w) in enumerate(blks):
                    P, bP = psS[cnt["S"] % 4], b_pS[cnt["S"] % 4]
                    cnt["S"] += 1
                    s.op(s.pe, lambda h: h.matmul(P[:, :w], lhsT=qT[i][cm * 64:(cm + 1) * 64, qi * 128:(qi + 1) * 128],
                                                  rhs=kT[i][cm * 64:(cm + 1) * 64, o:o + w], start=True, stop=True),
                         reads=[b_qkv[i]], writes=[bP])
                    last = (bi == len(blks) - 1)
                    wn = w - 128 if last else w
                    col = cm * 8 + bi

                    def ex(h):
                        ins = None
                        if wn > 0:
                            ins = h.activation(out=PA[:, o:o + wn], in_=P[:, :wn], func=AF.Exp, scale=scale_d,
                                               accum_out=sm[r][:, col:col + 1])
                        if last:
                            ins = h.activation(out=dg[r][:], in_=P[:, wn:w], func=AF.Exp, scale=scale_d)
                        return ins

                    s.op(s.act, ex, reads=[bP], writes=[b_A[r]])
                    if last:
                        s.chain(s.dve, [
                            lambda h: h.tensor_tensor(out=PA[:, nk - 128:nk], in0=dg[r][:], in1=self.mask_le[:], op=ALU.mult),
                            lambda h: h.reduce_sum(out=sm[r][:, cm * 8 + 7:cm * 8 + 8], in_=PA[:, nk - 128:nk], axis=AX.X),
                        ], reads=[self.b_const], writes=[b_A[r]])

        def d_back(i, qi):
            r = qi % NR
            nk = (qi + 1) * 128
            def comb_a(h):
                h.reduce_sum(out=sm[r][:, 16:17], in_=sm[r][:, 0:8], axis=AX.X)
                return h.reduce_sum(out=sm[r][:, 17:18], in_=sm[r][:, 8:16], axis=AX.X)

            def comb_c(h):
                h.tensor_tensor(out=sm[r][:, 20:21], in0=sm[r][:, 19:20], in1=neglam, op=ALU.mult)
                return h.tensor_scalar(out=A1[r][:, :nk], in0=A1[r][:, :nk], scalar1=sm[r][:, 18:19], scalar2=None,
                                       op0=ALU.mult)

            s.chain(s.dve, [
                comb_a,
                lambda h: h.reciprocal(out=sm[r][:, 18:20], in_=sm[r][:, 16:18]),
                comb_c,
                lambda h: h.scalar_tensor_tensor(out=Wb[r][:, :nk], in0=A2[r][:, :nk], scalar=sm[r][:, 20:21],
                                                 in1=A1[r][:, :nk], op0=ALU.mult, op1=ALU.add),
            ], reads=[b_par], writes=[b_A[r], b_Wb[r]])
            transposes_W(r, qi + 1)
            PO, bPO = psO[cnt["O"] % 2], b_pO[cnt["O"] % 2]
            cnt["O"] += 1

            def pv(h):
                last = None
                for kt in range(qi + 1):
                    last = h.matmul(PO[:, :], lhsT=WT[r][:, kt, :], rhs=Vtm[:, kt, :], start=(kt == 0), stop=(kt == qi))
                return last

            s.op(s.pe, pv, reads=[b_WT[r], b_V], writes=[bPO])
            s.op(s.dve, lambda h: h.memset(sm[r][:, 24:25], 0.0), writes=[b_on[r]])
            s.op(s.act, lambda h: h.activation(out=dg[r][:], in_=PO[:, :], func=AF.Square, accum_out=sm[r][:, 24:25]),
                 reads=[bPO], writes=[b_on[r], b_A[r]])

            s.op(s.dve, lambda h: h.tensor_scalar(out=sm[r][:, 25:26], in0=sm[r][:, 24:25], scalar1=1.0 / 128,
                                                  scalar2=LN_EPS, op0=ALU.mult, op1=ALU.add), writes=[b_on[r]])

            s.chain(s.act, [
                lambda h: h.activation(out=sm[r][:, 26:27], in_=sm[r][:, 25:26], func=AF.Ln),
                lambda h: h.activation(out=sm[r][:, 26:27], in_=sm[r][:, 26:27], func=AF.Exp, scale=-0.5),
            ], writes=[b_on[r]])
            s.op(s.dve, lambda h: h.scalar_tensor_tensor(out=onb[r][:], in0=PO[:, :], scalar=sm[r][:, 26:27], in1=gt[:],
                                                         op0=ALU.mult, op1=ALU.mult),
                 reads=[bPO, b_par], writes=[b_on[r]])
            P, bP = psT[cnt["T"] % 2], b_pT[cnt["T"] % 2]
            cnt["T"] += 1
            s.op(s.pe, lambda h: h.transpose(P[:, 0, :], onb[r][:], self.ident[:]), reads=[b_on[r], self.b_const],
                 writes=[bP])
            s.op(s.act, lambda h: h.copy(out=OT[i][:, qi * 128:(qi + 1) * 128], in_=P[:, 0, :]), reads=[bP],
                 writes=[b_OT[i]])

        def s_front(i, qi):
            r = qi % NR
            nk = (qi + 1) * 128
            blks = kblocks(nk)
            E, SP, ZS, CS = A1[r], A2[r], A3[r], A4[r]
            for bi, (o, w) in enumerate(blks):
                P, bP = psS[cnt["S"] % 4], b_pS[cnt["S"] % 4]
                cnt["S"] += 1
                s.op(s.pe, lambda h: h.matmul(P[:, :w], lhsT=qT[i][:, qi * 128:(qi + 1) * 128], rhs=kT[i][:, o:o + w],
                                              start=True, stop=True), reads=[b_qkv[i]], writes=[bP])
                s.op(s.act, lambda h: h.activation(out=E[:, o:o + w], in_=P[:, :w], func=AF.Exp, scale=scale_s),
                     reads=[bP], writes=[b_A[r]])
                s.op(s.dve, lambda h: h.tensor_scalar(out=ZS[:, o:o + w], in0=P[:, :w], scalar1=scale_s, scalar2=None,
                                                      op0=ALU.mult), reads=[bP], writes=[b_A[r]])
            s.op(s.act, lambda h: h.activation(out=SP[:, :nk], in_=E[:, :nk], func=AF.Ln, bias=1.0), writes=[b_A[r]])

            def scan_b(h):
                h.tensor_tensor_scan(out=CS[:, :nk], data0=self.ones[:, :nk], data1=SP[:, :nk], initial=0.0,
                                     op0=ALU.mult, op1=ALU.add)
                return h.tensor_tensor(out=ZS[:, :nk], in0=ZS[:, :nk], in1=SP[:, :nk], op=ALU.subtract)

            def scan_c(h):
                h.tensor_tensor(out=ZS[:, :nk], in0=ZS[:, :nk], in1=CS[:, :nk], op=ALU.add)
                return h.tensor_scalar(out=sm[r][:, 0:1], in0=CS[:, nk - 1:nk], scalar1=-1.0, scalar2=None, op0=ALU.mult)

            s.chain(s.dve, [
                lambda h: h.tensor_tensor(out=SP[:, nk - 128:nk], in0=SP[:, nk - 128:nk], in1=self.mask_lt[:], op=ALU.mult),
                scan_b, scan_c,
            ], reads=[self.b_const], writes=[b_A[r]])

        def s_back(i, qi):
            r = qi % NR
            nk = (qi + 1) * 128
            ZS = A3[r]
            def wexp(h):
                ins = None
                if nk > 128:
                    ins = h.activation(out=Wb[r][:, :nk - 128], in_=ZS[:, :nk - 128], func=AF.Exp, bias=sm[r][:, 0:1])
                return h.activation(out=dg[r][:], in_=ZS[:, nk - 128:nk], func=AF.Exp, bias=sm[r][:, 0:1])

            s.op(s.act, wexp, writes=[b_A[r], b_Wb[r]])
            s.op(s.dve, lambda h: h.tensor_tensor(out=Wb[r][:, nk - 128:nk], in0=dg[r][:], in1=self.mask_lt[:], op=ALU.mult),
                 reads=[self.b_const], writes=[b_A[r], b_Wb[r]])
            transposes_W(r, qi + 1)
            PO, bPO = psO[cnt["O"] % 2], b_pO[cnt["O"] % 2]
            cnt["O"] += 1

            def pv(h):
                last = None
                for kt in range(qi + 1):
                    last = h.matmul(PO[:, :], lhsT=Vtm[:, kt, :], rhs=WT[r][:, kt, :], start=(kt == 0), stop=(kt == qi))
                return last

            s.op(s.pe, pv, reads=[b_WT[r], b_V], writes=[bPO])
            s.op(s.act, lambda h: h.copy(out=OT[i][:, qi * 128:(qi + 1) * 128], in_=PO[:, :]), reads=[bPO],
                 writes=[b_OT[i]])

        def run_rows(front, back, i):
            front(i, 0)
            for qi in range(NT):
                if qi + 1 < NT:
                    front(i, qi + 1)
                back(i, qi)

        hi = 0
        for hd in range(c.HD):
            i = hi % NX
            hi += 1
            load_head(i, 0 * c.HD + hd, 1 * c.HD + hd, 2 * c.HD + hd)
            build_V(i)
            run_rows(d_front, d_back, i)
            self.mix_store(OT[i], b_OT[i], c.CCH + hd)
        for hs in range(c.HS):
            i = hi % NX
            hi += 1
            base = 3 * c.HD
            load_head(i, base + 0 * c.HS + hs, base + 1 * c.HS + hs, base + 2 * c.HS + hs)
            build_V(i)
            run_rows(s_front, s_back, i)
            self.mix_store(OT[i], b_OT[i], c.CCH + c.HD + hs)

    def gemm_T(self, es_outer, XT, b_XT, nk, Wd, row0, bl, res_src, first):
        s, c = self.s, self.c
        D, TB = c.D, c.TB
        with ExitStack() as es:
            wb = [self.sb(es, [128, c.KC, 512], BF16, "wT") for _ in range(2)]
            b_w = [Buf("w") for _ in range(2)]
            pb = [self.ps(es, [128, 512], F32, "pT") for _ in range(6)]
            b_p = [Buf("p") for _ in range(6)]
            NS = 4
            rb = [self.sb(es, [128, 512], F32, "rT") for _ in range(NS)]
            yb = [self.sb(es, [128, 512], F32, "yT") for _ in range(NS)]
            b_r = [Buf("r") for _ in range(NS)]
            b_y = [Buf("y") for _ in range(NS)]
            b_dr = {}
            pi = 0
            si = 0
            for nb in range(D // 512):
                W, bW = wb[nb % 2], b_w[nb % 2]
                self.load_w(W, bW, Wd, row0, nk, nb * 512, 512)
                for (off, n) in c.mtiles:
                    P, bP = pb[pi % 6], b_p[pi % 6]
                    pi += 1
                    RB, YB, bR, bY = rb[si % NS], yb[si % NS], b_r[si % NS], b_y[si % NS]
                    si += 1
                    r0 = bl * TB + off
                    s.dma(s.q_sync, RB[:n, :], res_src[r0:r0 + n, nb * 512:(nb + 1) * 512], writes=[bR])

                    def mm(h):
                        last = None
                        for k in range(nk):
                            last = h.matmul(P[:n, :], lhsT=XT[:, k, off:off + n], rhs=W[:, k, :], start=(k == 0),
                                            stop=(k == nk - 1))
                        return last

                    s.op(s.pe, mm, reads=[bW, b_XT], writes=[bP])
                    sc = c.alpha if first else 1.0
                    s.op(s.dve, lambda h: h.scalar_tensor_tensor(out=YB[:n, :], in0=RB[:n, :], scalar=sc, in1=P[:n, :],
                                                                 op0=ALU.mult, op1=ALU.add),
                         reads=[bR, bP], writes=[bY])
                    s.dma(s.q_sync, self.ypre[r0:r0 + n, nb * 512:(nb + 1) * 512], YB[:n, :], reads=[bY])
            s.barrier()

    def stage_wup(self, l, XT, b_XT, XTh, b_XTh, blk):
        s, c = self.s, self.c
        D, KC, TB, DFF, FC = c.D, c.KC, c.TB, c.DFF, c.FC
        with ExitStack() as es:
            fw = self.sb(es, [128, 2 * FC * 3], F32, "fcw")
            b_fw = Buf("fcw")
            s.dma(s.q_sync, fw[:], self.fcw[l * 128:(l + 1) * 128, :], writes=[b_fw])
            wgu = [self.sb(es, [128, KC, 512], BF16, "wgu") for _ in range(2)]
            b_w = [Buf("w") for _ in range(2)]
            pb = [self.ps(es, [128, 512], F32, "pF") for _ in range(6)]
            b_p = [Buf("p") for _ in range(6)]
            ph = [self.ps(es, [128, 16], F32, "pH") for _ in range(2)]
            b_ph = [Buf("ph") for _ in range(2)]
            NS = 2
            ug = [self.sb(es, [128, TB + 2], F32, "ug") for _ in range(NS)]
            uu = [self.sb(es, [128, TB + 2], F32, "uu") for _ in range(NS)]
            cg = [self.sb(es, [128, TB], F32, "cg") for _ in range(NS)]
            cu = [self.sb(es, [128, TB], F32, "cu") for _ in range(NS)]
            ab = [self.sb(es, [128, TB], BF16, "ab") for _ in range(NS)]
            b_ug = [Buf("ug") for _ in range(NS)]
            b_uu = [Buf("uu") for _ in range(NS)]
            b_cg = [Buf("cg") for _ in range(NS)]
            b_cu = [Buf("cu") for _ in range(NS)]
            b_ab = [Buf("ab") for _ in range(NS)]
            pi = 0
            hi = 0
            for g in range(FC // 2):
                WGU, bW = wgu[g % 2], b_w[g % 2]
                self.load_w(WGU, bW, self.w_up, l * D, KC, g * 512, 512)
                for j in range(2):
                    ch = g * 2 + j
                    i = ch % NS
                    for (W, U, bU) in ((WGU[:, :, 0:256], ug[i], b_ug[i]), (WGU[:, :, 256:512], uu[i], b_uu[i])):
                        if blk == 0:
                            s.op(s.dve, lambda h: h.memset(U[:, 0:2], 0.0), writes=[bU])
                        else:
                            PH, bPH = ph[hi % 2], b_ph[hi % 2]
                            hi += 1

                            def mmh(h):
                                last = None
                                for k in range(KC):
                                    last = h.matmul(PH[:, 0:2], lhsT=W[:, k, j * 128:(j + 1) * 128], rhs=XTh[:, k, 0:2],
                                                    start=(k == 0), stop=(k == KC - 1))
                                return last

                            s.op(s.pe, mmh, reads=[bW, b_XTh], writes=[bPH])
                            s.op(s.act, lambda h: h.copy(out=U[:, 0:2], in_=PH[:, 0:2]), reads=[bPH], writes=[bU])
                        for (o, n) in c.tsubs:
                            P, bP = pb[pi % 6], b_p[pi % 6]
                            pi += 1

                            def mm(h):
                                last = None
                                for k in range(KC):
                                    last = h.matmul(P[:, :n], lhsT=W[:, k, j * 128:(j + 1) * 128], rhs=XT[:, k, o:o + n],
                                                    start=(k == 0), stop=(k == KC - 1))
                                return last

                            s.op(s.pe, mm, reads=[bW, b_XT], writes=[bP])
                            s.op(s.act, lambda h: h.copy(out=U[:, 2 + o:2 + o + n], in_=P[:, :n]), reads=[bP], writes=[bU])
                    UG, UU, CG, CU, AB = ug[i], uu[i], cg[i], cu[i], ab[i]
                    wcol = lambda cidx, k: fw[:, cidx * 3 + k:cidx * 3 + k + 1]

                    def cv0(h):
                        h.tensor_scalar(out=CG[:], in0=UG[:, 2:], scalar1=wcol(ch, 2), scalar2=None, op0=ALU.mult)
                        return h.tensor_scalar(out=CU[:], in0=UU[:, 2:], scalar1=wcol(FC + ch, 2), scalar2=None, op0=ALU.mult)

                    def cvk(k):
                        def f(h):
                            h.scalar_tensor_tensor(out=CG[:], in0=UG[:, k:TB + k], scalar=wcol(ch, k), in1=CG[:],
                                                   op0=ALU.mult, op1=ALU.add)
                            return h.scalar_tensor_tensor(out=CU[:], in0=UU[:, k:TB + k], scalar=wcol(FC + ch, k), in1=CU[:],
                                                          op0=ALU.mult, op1=ALU.add)
                        return f

                    s.chain(s.dve, [cv0, cvk(1), cvk(0)], reads=[b_ug[i], b_uu[i], b_fw], writes=[b_cg[i], b_cu[i]])
                    s.op(s.act, lambda h: h.activation(out=CG[:], in_=CG[:], func=AF.Silu), writes=[b_cg[i]])
                    s.op(s.dve, lambda h: h.tensor_tensor(out=AB[:], in0=CG[:], in1=CU[:], op=ALU.mult),
                         reads=[b_cg[i], b_cu[i]], writes=[b_ab[i]])
                    s.dma(s.q_sync, self.aT[ch * 128:(ch + 1) * 128, :], AB[:], reads=[b_ab[i]])
            s.barrier()

    def stage_dense(self, l):
        s, c = self.s, self.c
        D, KC, TB = c.D, c.KC, c.TB
        last_layer = (l == c.DEPTH - 1)
        with ExitStack() as es:
            XT = self.sb(es, [128, KC, TB], BF16, "XTd")
            b_XT = Buf("XT")
            XTh = self.sb(es, [128, KC, 2], BF16, "XTh")
            b_XTh = Buf("XTh")
            for bl in range(c.NBL):
                blk = bl
                s.dma(s.q_sync, XT[:, :, :], self.mixT[blk * c.MCL * 128:(blk + 1) * c.MCL * 128, :]
                      .rearrange("(k p) t -> p k t", p=128), writes=[b_XT])
                self.gemm_T(es, XT, b_XT, KC, self.w_out, l * D, bl, self.hres, True)
                self.ln_pass(es, self.ypre, self.ln1g[l:l + 1, :], self.ln1b[l:l + 1, :], XT, b_XT, bl, h_dst=self.hres)
                self.stage_wup(l, XT, b_XT, XTh, b_XTh, blk)
                if bl + 1 < c.NBL:
                    s.op(s.dve, lambda h: h.tensor_copy(out=XTh[:, :, :], in_=XT[:, :, TB - 2:TB]), reads=[b_XT],
                         writes=[b_XTh])
                    s.barrier()
                for pi, (f0, fn) in enumerate(c.fparts):
                    s.dma(s.q_sync, XT[:, :fn, :], self.aT[f0 * 128:(f0 + fn) * 128, :].rearrange("(k p) t -> p k t", p=128),
                          writes=[b_XT])
                    self.gemm_T(es, XT, b_XT, fn, self.w_down, l * c.DFF + f0 * 128, bl,
                                self.hres if pi == 0 else self.ypre, pi == 0)
                if last_layer:
                    self.ln_pass(es, self.ypre, self.ln2g[l:l + 1, :], self.ln2b[l:l + 1, :], None, None, bl, h_dst=self.out,
                                 dst_off=self.sq * c.NBL * TB)
                else:
                    dst = self.hnT[blk * KC * 128:(blk + 1) * KC * 128, :]
                    self.ln_pass(es, self.ypre, self.ln2g[l:l + 1, :], self.ln2b[l:l + 1, :], XT, b_XT, bl,
                                 h_dst=self.hres, hnT_dst=dst)


def make_in_maps(cfg: Cfg, inp):
    c = cfg
    f = lambda a: np.ascontiguousarray(np.asarray(a, dtype=np.float32))
    x = f(inp["x"])
    meta = f(inp["meta_tokens"])
    D, DEPTH = c.D, c.DEPTH
    rows = c.mix_rows()
    w_out = (f(inp["w_out"]) if c.NPAIR == 1 else f(inp["w_out"])[:, rows, :]).reshape(DEPTH * D, D)
    w_up = np.ascontiguousarray(f(inp["w_up"]).reshape(DEPTH * D, 2, c.FC // 2, 256).transpose(0, 2, 1, 3)
                                .reshape(DEPTH * D, 2 * c.DFF))
    w_down = f(inp["w_down"]).reshape(DEPTH * c.DFF, D)
    fcw = f(inp["ffn_conv_w"]).reshape(DEPTH, 3, 2 * c.FC, 128).transpose(0, 3, 2, 1).reshape(DEPTH * 128, 2 * c.FC * 3)
    lamv = np.stack([f(inp["lambda_q1"]), f(inp["lambda_k1"]), f(inp["lambda_q2"]), f(inp["lambda_k2"])], axis=1)
    lamv = np.ascontiguousarray(lamv.reshape(DEPTH * 4, 64))
    common = {
        "embg": f(inp["emb_ln_g"]).reshape(1, D), "embb": f(inp["emb_ln_b"]).reshape(1, D),
        "lamv": lamv, "dng": f(inp["diff_norm_g"]),
        "w_out": np.ascontiguousarray(w_out), "ln1g": f(inp["ln1_g"]), "ln1b": f(inp["ln1_b"]),
        "w_up": w_up, "fcw": np.ascontiguousarray(fcw), "w_down": w_down,
        "ln2g": f(inp["ln2_g"]), "ln2b": f(inp["ln2_b"]),
    }
    per_rank = []
    for r in range(c.NPAIR):
        cols = c.in_cols(r)
        w_in = np.ascontiguousarray((f(inp["w_in"]) if c.NPAIR == 1 else f(inp["w_in"])[:, :, cols]).reshape(DEPTH * D, c.NCL * 128))
        scw = f(inp["short_conv_w"])[:, :, r * c.CCH * 128:(r + 1) * c.CCH * 128]
        scw = scw.reshape(DEPTH, 3, c.CCH, 128).transpose(0, 3, 2, 1).reshape(DEPTH * 128, c.CCH * 3)
        per_rank.append({"w_in": w_in, "scw": np.ascontiguousarray(scw)})
    maps = []
    for core in range(c.B // c.SPC):
        toks = np.zeros((c.SPC, c.T, D), np.float32)
        for j in range(c.SPC):
            toks[j, :N_META] = meta
            toks[j, N_META:c.L] = x[core * c.SPC + j]
        m = dict(common)
        m.update(per_rank[0])
        m["tok"] = toks.reshape(c.SPC * c.T, D)
        maps.append(m)
    return maps


def gather_out(cfg: Cfg, results):
    c = cfg
    out = np.zeros((c.B, c.SEQ, c.D), np.float32)
    for core in range(c.B // c.SPC):
        rows = np.asarray(results[core]["out"]).reshape(c.SPC, c.T, c.D)
        for j in range(c.SPC):
            out[core * c.SPC + j] = rows[j, N_META:c.L]
    return out


_CACHE = {}


def run(cfg: Cfg, inp):
    key = (cfg.D, cfg.SEQ, cfg.DEPTH, cfg.NPAIR, cfg.B, cfg.SPC)
    if key not in _CACHE:
        _CACHE[key] = Builder(cfg).build()
    nc = _CACHE[key]
    maps = make_in_maps(cfg, inp)
    res = run_bass_kernel_spmd(nc, maps, core_ids=list(range(len(maps))))
    return gather_out(cfg, res.results)


def kernel(**inputs):
    cfg = Cfg(D=4096, SEQ=2048, DEPTH=4, NPAIR=1, B=4, SPC=2)
    return run(cfg, inputs)
```

```python
import math
from contextlib import ExitStack

import numpy as np
import ml_dtypes
import concourse.bass as bass
import concourse.mybir as mybir
from concourse.bass_utils import run_bass_kernel_spmd

F32 = mybir.dt.float32
BF16 = mybir.dt.bfloat16
AF = mybir.ActivationFunctionType
ALU = mybir.AluOpType
AX = mybir.AxisListType

N_META = 16
LN_EPS = 1e-5
SAME_ENGINE_SYNC = True
DMA_POOL = 8


class Cfg:
    def __init__(self, D=4096, SEQ=2048, DEPTH=4, NPAIR=1, B=4, SPC=1):
        self.D, self.SEQ, self.DEPTH, self.NPAIR, self.B, self.SPC = D, SEQ, DEPTH, NPAIR, B, SPC
        assert NPAIR == 1 and B % SPC == 0
        self.L = N_META + SEQ
        self.NB = 2
        t = -(-self.L // 128) * 128
        if (t // 2) % 64:
            t += 128
        self.T = t
        self.TB = t // 2
        self.NBL = self.NB // NPAIR
        self.KC = D // 128
        self.CONV = D // 4
        self.NDH = (3 * D) // (8 * 128)
        self.NSH = self.NDH
        self.DFF = ((8 * D // 3 + 255) // 256) * 256
        self.FC = self.DFF // 128
        self.CCH = self.CONV // 128 // NPAIR
        self.HD = self.NDH // NPAIR
        self.HS = self.NSH // NPAIR
        self.NCL = 3 * self.CCH + 3 * self.HD + 3 * self.HS
        self.MCL = self.CCH + self.HD + self.HS
        self.alpha = (2 * DEPTH) ** 0.25
        assert self.MCL * NPAIR == self.KC
        self.mtiles = [(o, min(128, self.TB - o)) for o in range(0, self.TB, 128)]
        self.tsubs = [(o, min(512, self.TB - o)) for o in range(0, self.TB, 512)]
        nparts = -(-self.FC // self.KC)
        base = self.FC // nparts
        rem = self.FC % nparts
        self.fparts = []
        o = 0
        for i in range(nparts):
            n = base + (1 if i < rem else 0)
            self.fparts.append((o, n))
            o += n

    def lam_init(self, layer):
        return 0.8 - 0.6 * math.exp(-0.3 * layer)

    def in_cols(self, r):
        C, DW = self.CONV, self.NDH * 128
        cl, dl, sl = self.CCH * 128, self.HD * 128, self.HS * 128
        cols = []
        for i in range(3):
            cols.append(np.arange(i * C + r * cl, i * C + (r + 1) * cl))
        for i in range(3):
            cols.append(np.arange(3 * C + i * DW + r * dl, 3 * C + i * DW + (r + 1) * dl))
        for i in range(3):
            cols.append(np.arange(3 * C + 3 * DW + i * DW + r * sl, 3 * C + 3 * DW + i * DW + (r + 1) * sl))
        return np.concatenate(cols)

    def mix_rows(self):
        C, DW = self.CONV, self.NDH * 128
        cl, dl, sl = self.CCH * 128, self.HD * 128, self.HS * 128
        rows = []
        for r in range(self.NPAIR):
            rows.append(np.arange(r * cl, (r + 1) * cl))
            rows.append(np.arange(C + r * dl, C + (r + 1) * dl))
            rows.append(np.arange(C + DW + r * sl, C + DW + (r + 1) * sl))
        return np.concatenate(rows)


class Buf:
    __slots__ = ("name", "w", "r")

    def __init__(self, name):
        self.name = name
        self.w = None
        self.r = {}


class Eng:
    def __init__(self, name, h, sem, sync_self):
        self.name, self.h, self.sem, self.cnt = name, h, sem, 0
        self.waited = {}
        self.sync_self = sync_self


class DmaQ:
    def __init__(self, eng, sems):
        self.eng, self.sems, self.idx = eng, sems, 0


class Sched:
    def __init__(self, nc, es):
        self.nc = nc
        mk = lambda n: es.enter_context(nc.semaphore(n))
        self.pe = Eng("pe", nc.tensor, mk("s_pe"), False)
        self.act = Eng("act", nc.scalar, mk("s_act"), SAME_ENGINE_SYNC)
        self.dve = Eng("dve", nc.vector, mk("s_dve"), SAME_ENGINE_SYNC)
        self.pool = Eng("pool", nc.gpsimd, mk("s_pool"), SAME_ENGINE_SYNC)
        self.sp = Eng("sp", nc.sync, mk("s_sp"), False)
        self.engs = [self.pe, self.act, self.dve, self.pool, self.sp]
        self.q_sync = DmaQ(self.sp, [mk(f"s_dq{i}") for i in range(DMA_POOL)])
        self.q_pool = DmaQ(self.pool, [mk(f"s_dp{i}") for i in range(DMA_POOL)])
        self.queues = [self.q_sync, self.q_pool]
        self.n_inst = 0

    def _deps(self, reads, writes):
        d = {}

        def add(ev):
            if ev is not None:
                s, v = ev
                if d.get(s, (None, 0))[1] < v:
                    d[s] = (s, v)

        for b in reads:
            add(b.w)
        for b in writes:
            add(b.w)
            for s, v in b.r.items():
                add((s, v))
        return list(d.values())

    def _wait(self, eng, deps):
        for s, v in deps:
            if s is eng.sem and not eng.sync_self:
                continue
            if eng.waited.get(s, 0) < v:
                eng.h.wait_ge(s, v)
                eng.waited[s] = v
                self.n_inst += 1

    def _commit(self, ev, reads, writes):
        s, v = ev
        for b in writes:
            b.w = ev
            b.r = {}
        for b in reads:
            if b.r.get(s, 0) < v:
                b.r[s] = v

    def op(self, eng, emit, reads=(), writes=()):
        self._wait(eng, self._deps(reads, writes))
        inst = emit(eng.h)
        eng.cnt += 1
        inst.then_inc(eng.sem, 1)
        self.n_inst += 1
        self._commit((eng.sem, eng.cnt), reads, writes)

    def chain(self, eng, emits, reads=(), writes=()):
        for e in emits:
            self.op(eng, e, reads=reads, writes=writes)

    def dma(self, q, out, in_, reads=(), writes=()):
        slot, rnd = q.idx % len(q.sems), q.idx // len(q.sems)
        q.idx += 1
        sem = q.sems[slot]
        deps = self._deps(reads, writes)
        if rnd > 0:
            deps.append((sem, 16 * rnd))
        self._wait(q.eng, deps)
        q.eng.h.dma_start(out=out, in_=in_).then_inc(sem, 16)
        self.n_inst += 1
        self._commit((sem, 16 * (rnd + 1)), reads, writes)

    def all_events(self):
        evs = [(e.sem, e.cnt) for e in self.engs if e.cnt > 0]
        for q in self.queues:
            for i, s in enumerate(q.sems):
                n = (q.idx - i + len(q.sems) - 1) // len(q.sems)
                if n > 0:
                    evs.append((s, 16 * n))
        return evs

    def barrier(self, engs=None):
        evs = self.all_events()
        for e in (engs or self.engs):
            sv = e.sync_self
            e.sync_self = True
            self._wait(e, evs)
            e.sync_self = sv


class Builder:
    def __init__(self, cfg: Cfg, debug=False):
        self.debug = debug
        self.c = cfg
        self.nc = bass.Bass("TRN2", target_bir_lowering=False)
        self._uid = 0

    def uid(self, p):
        self._uid += 1
        return f"{p}{self._uid}"

    def sb(self, es, shape, dt, name=None):
        return es.enter_context(self.nc.sbuf_tensor(self.uid(name or "sb"), list(shape), dt))

    def ps(self, es, shape, dt, name=None):
        return es.enter_context(self.nc.psum_tensor(self.uid(name or "ps"), list(shape), dt))

    def dram(self, name, shape, dt, kind="Internal"):
        if kind == "Internal" and self.debug:
            kind = "ExternalOutput"
        return self.nc.dram_tensor(name, list(shape), dt, kind=kind).ap()

    def build(self):
        c, nc = self.c, self.nc
        D, T, TB, KC, DEPTH = c.D, c.T, c.TB, c.KC, c.DEPTH
        R = c.NBL * TB
        ein = lambda n, s: self.dram(n, s, F32, kind="ExternalInput")
        self.tok = ein("tok", [c.SPC * R, D])
        self.embg = ein("embg", [1, D])
        self.embb = ein("embb", [1, D])
        self.w_in = ein("w_in", [DEPTH * D, c.NCL * 128])
        self.scw = ein("scw", [DEPTH * 128, c.CCH * 3])
        self.lamv = ein("lamv", [DEPTH * 4, 64])
        self.dng = ein("dng", [DEPTH, 128])
        self.w_out = ein("w_out", [DEPTH * D, D])
        self.ln1g = ein("ln1g", [DEPTH, D])
        self.ln1b = ein("ln1b", [DEPTH, D])
        self.w_up = ein("w_up", [DEPTH * D, 2 * c.DFF])
        self.fcw = ein("fcw", [DEPTH * 128, 2 * c.FC * 3])
        self.w_down = ein("w_down", [DEPTH * c.DFF, D])
        self.ln2g = ein("ln2g", [DEPTH, D])
        self.ln2b = ein("ln2b", [DEPTH, D])
        self.out = self.dram("out", [c.SPC * R, D], F32, kind="ExternalOutput")
        self.hres = self.dram("hres", [R, D], F32)
        self.ypre = self.dram("ypre", [R, D], F32)
        self.hnT = self.dram("hnT", [c.NB * KC * 128, TB], BF16)
        self.qkvT = self.dram("qkvT", [(c.NCL - 3 * c.CCH) * 128, T], BF16)
        self.cvT = self.dram("cvT", [3 * c.CCH * 128, T], F32)
        self.mixT = self.dram("mixT", [c.NB * c.MCL * 128, TB], BF16)
        self.aT = self.dram("aT", [c.FC * 128, TB], BF16)

        with ExitStack() as es:
            self.s = Sched(nc, es)
            self.consts(es)
            self.s.barrier()
            for sq in range(c.SPC):
                self.sq = sq
                self.stage_embed()
                for l in range(DEPTH):
                    self.stage_win(l)
                    self.stage_mixers(l)
                    self.stage_dense(l)
            self.s.barrier()
        return nc

    def consts(self, es):
        s, c = self.s, self.c
        self.ident = self.sb(es, [128, 128], BF16, "ident")
        self.mask_le = self.sb(es, [128, 128], F32, "mle")
        self.mask_lt = self.sb(es, [128, 128], F32, "mlt")
        self.ones = self.sb(es, [128, c.T], F32, "ones")
        self.b_const = Buf("const")
        g = self.nc.gpsimd

        sel = lambda t, op: (lambda h: h.affine_select(out=t[:], in_=t[:], pattern=[[-1, 128]], compare_op=op,
                                                       fill=0.0, base=0, channel_multiplier=1))
        s.chain(s.pool, [
            lambda h: h.memset(self.ident[:], 1.0), sel(self.ident, ALU.is_equal),
            lambda h: h.memset(self.mask_le[:], 1.0), sel(self.mask_le, ALU.is_ge),
            lambda h: h.memset(self.mask_lt[:], 1.0), sel(self.mask_lt, ALU.is_gt),
            lambda h: h.memset(self.ones[:], 1.0),
        ], writes=[self.b_const])

    def ln_pass(self, es, src, g_ap, b_ap, XT, b_XT, blk_local, h_dst=None, hnT_dst=None, src_off=0, dst_off=0):
        s, c, nc = self.s, self.c, self.nc
        D, KC, TB = c.D, c.KC, c.TB
        with ExitStack() as st:
            gt = self.sb(st, [128, D], F32, "lng")
            bt = self.sb(st, [128, D], F32, "lnb")
            b_gb = Buf("gb")
            s.dma(s.q_sync, gt[:], g_ap.partition_broadcast(128), writes=[b_gb])
            s.dma(s.q_sync, bt[:], b_ap.partition_broadcast(128), writes=[b_gb])
            NX = 2
            xt = [self.sb(st, [128, D], F32, "lnx") for _ in range(NX)]
            hb = [self.sb(st, [128, D], BF16, "lnhb") for _ in range(NX)]
            st4 = [self.sb(st, [128, 8], F32, "lnst") for _ in range(NX)]
            junk = self.sb(st, [128, D], BF16, "lnjunk")
            b_x = [Buf("lnx") for _ in range(NX)]
            b_hb = [Buf("lnhb") for _ in range(NX)]
            b_st = [Buf("lnst") for _ in range(NX)]
            b_junk = Buf("junk")
            pt = [self.ps(st, [128, 8, 128], BF16, "lnpt") for _ in range(2)]
            b_pt = [Buf("lnpt") for _ in range(2)]
            pti = 0
            for ti, (off, n) in enumerate(c.mtiles):
                i = ti % NX
                r0 = blk_local * TB + off
                X, HB, ST = xt[i], hb[i], st4[i]
                s.dma(s.q_sync, X[:n, :], src[src_off + r0:src_off + r0 + n, :], writes=[b_x[i]])
                s.op(s.dve, lambda h: h.memset(ST[:n, :], 0.0), writes=[b_st[i]])
                s.op(s.act, lambda h: h.activation(out=junk[:n, :], in_=X[:n, :], func=AF.Identity,
                                                   accum_out=ST[:n, 0:1]),
                     reads=[b_x[i]], writes=[b_junk, b_st[i]])
                s.op(s.act, lambda h: h.activation(out=junk[:n, :], in_=X[:n, :], func=AF.Square,
                                                   accum_out=ST[:n, 1:2]),
                     reads=[b_x[i]], writes=[b_junk, b_st[i]])

                s.chain(s.dve, [
                    lambda h: h.tensor_scalar(out=ST[:n, 2:3], in0=ST[:n, 0:1], scalar1=1.0 / D, scalar2=None, op0=ALU.mult),
                    lambda h: h.tensor_tensor(out=ST[:n, 4:5], in0=ST[:n, 2:3], in1=ST[:n, 2:3], op=ALU.mult),
                    lambda h: h.scalar_tensor_tensor(out=ST[:n, 5:6], in0=ST[:n, 1:2], scalar=1.0 / D, in1=ST[:n, 4:5],
                                                     op0=ALU.mult, op1=ALU.subtract),
                    lambda h: h.tensor_scalar(out=ST[:n, 5:6], in0=ST[:n, 5:6], scalar1=LN_EPS, scalar2=None, op0=ALU.add),
                ], writes=[b_st[i]])
                s.chain(s.act, [
                    lambda h: h.activation(out=ST[:n, 6:7], in_=ST[:n, 5:6], func=AF.Ln),
                    lambda h: h.activation(out=ST[:n, 6:7], in_=ST[:n, 6:7], func=AF.Exp, scale=-0.5),
                ], writes=[b_st[i]])
                s.op(s.dve, lambda h: h.scalar_tensor_tensor(out=ST[:n, 7:8], in0=ST[:n, 2:3], scalar=-1.0,
                                                             in1=ST[:n, 6:7], op0=ALU.mult, op1=ALU.mult),
                     writes=[b_st[i]])
                s.op(s.act, lambda h: h.activation(out=X[:n, :], in_=X[:n, :], func=AF.Identity,
                                                   scale=ST[:n, 6:7], bias=ST[:n, 7:8]),
                     reads=[b_st[i]], writes=[b_x[i]])
                s.op(s.dve, lambda h: h.tensor_tensor(out=X[:n, :], in0=X[:n, :], in1=gt[:n, :], op=ALU.mult),
                     reads=[b_gb], writes=[b_x[i]])
                s.op(s.dve, lambda h: h.tensor_tensor(out=X[:n, :], in0=X[:n, :], in1=bt[:n, :], op=ALU.add),
                     reads=[b_gb], writes=[b_x[i]])
                if h_dst is not None:
                    s.dma(s.q_sync, h_dst[dst_off + r0:dst_off + r0 + n, :], X[:n, :], reads=[b_x[i]])
                if XT is not None:
                    s.op(s.pool, lambda h: h.tensor_copy(out=HB[:n, :], in_=X[:n, :]), reads=[b_x[i]], writes=[b_hb[i]])
                    for k0 in range(0, KC, 8):
                        kn = min(8, KC - k0)
                        P, bP = pt[pti % 2], b_pt[pti % 2]
                        pti += 1

                        def tr(h):
                            last = None
                            for k in range(kn):
                                last = h.transpose(P[:, k, :n], HB[:n, (k0 + k) * 128:(k0 + k + 1) * 128],
                                                   self.ident[:n, :n])
                            return last

                        s.op(s.pe, tr, reads=[b_hb[i], self.b_const], writes=[bP])
                        eng = s.act if (k0 // 8) % 2 == 0 else s.dve
                        if eng is s.act:
                            s.op(eng, lambda h: h.copy(out=XT[:, k0:k0 + kn, off:off + n], in_=P[:, :kn, :n]),
                                 reads=[bP], writes=[b_XT])
                        else:
                            s.op(eng, lambda h: h.tensor_copy(out=XT[:, k0:k0 + kn, off:off + n], in_=P[:, :kn, :n]),
                                 reads=[bP], writes=[b_XT])
            if hnT_dst is not None:
                s.dma(s.q_sync, hnT_dst.rearrange("(k p) t -> p k t", p=128), XT[:, :KC, :TB], reads=[b_XT])
            s.barrier()

    def stage_embed(self):
        s, c = self.s, self.c
        with ExitStack() as es:
            XT = self.sb(es, [128, c.KC, c.TB], BF16, "XTe")
            b_XT = Buf("XT")
            for bl in range(c.NBL):
                blk = bl
                dst = self.hnT[blk * c.KC * 128:(blk + 1) * c.KC * 128, :]
                self.ln_pass(es, self.tok, self.embg[0:1, :], self.embb[0:1, :], XT, b_XT, bl,
                             h_dst=self.hres, hnT_dst=dst, src_off=self.sq * c.NBL * c.TB)

    def load_w(self, wbuf3, bW, Wd, row0, nk, col0, ncols):
        s = self.s
        for k0 in range(0, nk, 8):
            kn = min(8, nk - k0)
            src = Wd[row0 + k0 * 128: row0 + (k0 + kn) * 128, col0:col0 + ncols].rearrange("(k p) c -> p k c", p=128)
            s.dma(s.q_pool, wbuf3[:, k0:k0 + kn, :ncols], src, writes=[bW])

    def stage_win(self, l):
        s, c = self.s, self.c
        D, KC, TB, T = c.D, c.KC, c.TB, c.T
        NG = -(-c.NCL // 4)
        with ExitStack() as es:
            XT = self.sb(es, [128, KC, TB], BF16, "XT1")
            b_XT = Buf("XT")
            wb = [self.sb(es, [128, KC, 512], BF16, "w1") for _ in range(2)]
            b_w = [Buf("w") for _ in range(2)]
            pb = [self.ps(es, [128, 512], F32, "p1") for _ in range(6)]
            b_p = [Buf("p") for _ in range(6)]
            NS = 4
            stb = [self.sb(es, [128, TB], BF16, "st1b") for _ in range(NS)]
            stf = [self.sb(es, [128, TB], F32, "st1f") for _ in range(NS)]
            b_st = [Buf("st") for _ in range(NS)]
            pi = 0
            si = 0
            gi = 0
            for blk in range(c.NB):
                s.dma(s.q_sync, XT[:, :, :], self.hnT[blk * KC * 128:(blk + 1) * KC * 128, :]
                      .rearrange("(k p) t -> p k t", p=128), writes=[b_XT])
                for g in range(NG):
                    W, bW = wb[gi % 2], b_w[gi % 2]
                    gi += 1
                    ncols = min(512, c.NCL * 128 - g * 512)
                    self.load_w(W, bW, self.w_in, l * D, KC, g * 512, ncols)
                    for j in range(ncols // 128):
                        ch = g * 4 + j
                        isconv = ch < 3 * c.CCH
                        ST = (stf if isconv else stb)[si % NS]
                        bS = b_st[si % NS]
                        si += 1
                        for (o, n) in c.tsubs:
                            P, bP = pb[pi % 6], b_p[pi % 6]
                            pi += 1

                            def mm(h):
                                last = None
                                for k in range(KC):
                                    last = h.matmul(P[:, :n], lhsT=W[:, k, j * 128:(j + 1) * 128], rhs=XT[:, k, o:o + n],
                                                    start=(k == 0), stop=(k == KC - 1))
                                return last

                            s.op(s.pe, mm, reads=[bW, b_XT], writes=[bP])
                            s.op(s.act, lambda h: h.copy(out=ST[:, o:o + n], in_=P[:, :n]), reads=[bP], writes=[bS])
                        if isconv:
                            dst = self.cvT[ch * 128:(ch + 1) * 128, blk * TB:(blk + 1) * TB]
                        else:
                            q = ch - 3 * c.CCH
                            dst = self.qkvT[q * 128:(q + 1) * 128, blk * TB:(blk + 1) * TB]
                        s.dma(s.q_sync, dst, ST[:, :], reads=[bS])
            s.barrier()

    def mix_store(self, OT, bO, chunk):
        s, c = self.s, self.c
        for blk in range(c.NB):
            r = (blk * c.MCL + chunk) * 128
            s.dma(s.q_sync, self.mixT[r:r + 128, :], OT[:, blk * c.TB:(blk + 1) * c.TB], reads=[bO])

    def stage_mixers(self, l):
        s, c, nc = self.s, self.c, self.nc
        T = c.T
        NT = T // 128
        with ExitStack() as es:
            wt = self.sb(es, [128, c.CCH * 3], F32, "scw")
            b_wt = Buf("scw")
            s.dma(s.q_sync, wt[:], self.scw[l * 128:(l + 1) * 128, :], writes=[b_wt])
            NX = 2
            cb = [self.sb(es, [128, T], F32, "cb") for _ in range(NX)]
            cc = [self.sb(es, [128, T], F32, "cc") for _ in range(NX)]
            zb = [self.sb(es, [128, T + 2], F32, "zb") for _ in range(NX)]
            yb = [self.sb(es, [128, T], F32, "yb") for _ in range(NX)]
            ob = [self.sb(es, [128, T], BF16, "ob") for _ in range(NX)]
            b_cb = [Buf("cb") for _ in range(NX)]
            b_cc = [Buf("cc") for _ in range(NX)]
            b_zb = [Buf("zb") for _ in range(NX)]
            b_yb = [Buf("yb") for _ in range(NX)]
            b_ob = [Buf("ob") for _ in range(NX)]
            for ch in range(c.CCH):
                i = ch % NX
                CB, CC, ZB, YB, OB = cb[i], cc[i], zb[i], yb[i], ob[i]
                s.dma(s.q_sync, CB[:], self.cvT[ch * 128:(ch + 1) * 128, :], writes=[b_cb[i]])
                s.dma(s.q_sync, CC[:], self.cvT[(c.CCH + ch) * 128:(c.CCH + ch + 1) * 128, :], writes=[b_cc[i]])
                s.dma(s.q_sync, ZB[:, 2:], self.cvT[(2 * c.CCH + ch) * 128:(2 * c.CCH + ch + 1) * 128, :],
                      writes=[b_zb[i]])

                s.chain(s.dve, [
                    lambda h: h.memset(ZB[:, 0:2], 0.0),
                    lambda h: h.tensor_tensor(out=ZB[:, 2:], in0=ZB[:, 2:], in1=CC[:], op=ALU.mult),
                    lambda h: h.tensor_scalar(out=YB[:], in0=ZB[:, 2:], scalar1=wt[:, ch * 3 + 2:ch * 3 + 3], scalar2=None,
                                              op0=ALU.mult),
                    lambda h: h.scalar_tensor_tensor(out=YB[:], in0=ZB[:, 1:T + 1], scalar=wt[:, ch * 3 + 1:ch * 3 + 2],
                                                     in1=YB[:], op0=ALU.mult, op1=ALU.add),
                    lambda h: h.scalar_tensor_tensor(out=YB[:], in0=ZB[:, 0:T], scalar=wt[:, ch * 3:ch * 3 + 1],
                                                     in1=YB[:], op0=ALU.mult, op1=ALU.add),
                    lambda h: h.tensor_tensor(out=OB[:], in0=YB[:], in1=CB[:], op=ALU.mult),
                ], reads=[b_cb[i], b_cc[i], b_wt], writes=[b_zb[i], b_yb[i], b_ob[i]])
                self.mix_store(OB, b_ob[i], ch)
            s.barrier()
        with ExitStack() as es:
            self.attn_heads(es, l)
            s.barrier()

    def attn_heads(self, es, l):
        s, c, nc = self.s, self.c, self.nc
        T = c.T
        NT = T // 128
        lam_init = c.lam_init(l)
        lq = self.sb(es, [128, 4, 64], F32, "lq")
        lam = self.sb(es, [128, 8], F32, "lam")
        gt = self.sb(es, [128, 128], F32, "dng")
        b_par = Buf("par")
        for i in range(4):
            s.dma(s.q_sync, lq[:, i, :], self.lamv[l * 4 + i:l * 4 + i + 1, :].partition_broadcast(128), writes=[b_par])
        s.dma(s.q_sync, gt[:], self.dng[l:l + 1, :].partition_broadcast(128), writes=[b_par])

        def lam_mul(h):
            h.tensor_tensor(out=lq[:, 0, :], in0=lq[:, 0, :], in1=lq[:, 1, :], op=ALU.mult)
            return h.tensor_tensor(out=lq[:, 2, :], in0=lq[:, 2, :], in1=lq[:, 3, :], op=ALU.mult)

        def lam_red(h):
            h.reduce_sum(out=lam[:, 0:1], in_=lq[:, 0, :], axis=AX.X)
            return h.reduce_sum(out=lam[:, 1:2], in_=lq[:, 2, :], axis=AX.X)

        s.chain(s.dve, [lam_mul, lam_red], writes=[b_par])
        s.op(s.act, lambda h: h.activation(out=lam[:, 2:4], in_=lam[:, 0:2], func=AF.Exp), writes=[b_par])
        s.chain(s.dve, [
            lambda h: h.tensor_tensor(out=lam[:, 4:5], in0=lam[:, 2:3], in1=lam[:, 3:4], op=ALU.subtract),
            lambda h: h.tensor_scalar(out=lam[:, 5:6], in0=lam[:, 4:5], scalar1=lam_init, scalar2=-1.0, op0=ALU.add,
                                      op1=ALU.mult),
            lambda h: h.tensor_scalar(out=gt[:], in0=gt[:], scalar1=1.0 - lam_init, scalar2=None, op0=ALU.mult),
        ], writes=[b_par])
        neglam = lam[:, 5:6]

        NX = 2
        qT = [self.sb(es, [128, T], BF16, "qT") for _ in range(NX)]
        kT = [self.sb(es, [128, T], BF16, "kT") for _ in range(NX)]
        vT = [self.sb(es, [128, T], BF16, "vT") for _ in range(NX)]
        b_qkv = [Buf("qkv") for _ in range(NX)]
        Vtm = self.sb(es, [128, NT, 128], BF16, "Vtm")
        b_V = Buf("Vtm")
        OT = [self.sb(es, [128, T], BF16, "OT") for _ in range(NX)]
        b_OT = [Buf("OT") for _ in range(NX)]
        NR = 2
        A1 = [self.sb(es, [128, T], F32, "A1") for _ in range(NR)]
        A2 = [self.sb(es, [128, T], F32, "A2") for _ in range(NR)]
        A3 = [self.sb(es, [128, T], F32, "A3") for _ in range(NR)]
        A4 = [self.sb(es, [128, T], F32, "A4") for _ in range(NR)]
        Wb = [self.sb(es, [128, T], BF16, "Wb") for _ in range(NR)]
        WT = [self.sb(es, [128, NT, 128], BF16, "WT") for _ in range(NR)]
        sm = [self.sb(es, [128, 32], F32, "sm") for _ in range(NR)]
        dg = [self.sb(es, [128, 128], F32, "dg") for _ in range(NR)]
        onb = [self.sb(es, [128, 128], BF16, "onb") for _ in range(NR)]
        OQ = self.sb(es, [128, NT, 128], F32, "OQ")
        S1v = self.sb(es, [128, 128], F32, "S1v")
        b_OQ = Buf("OQ")
        b_A = [Buf("A") for _ in range(NR)]
        b_Wb = [Buf("Wb") for _ in range(NR)]
        b_WT = [Buf("WT") for _ in range(NR)]
        b_on = [Buf("on") for _ in range(NR)]
        psS = [self.ps(es, [128, 512], F32, "psS") for _ in range(4)]
        b_pS = [Buf("pS") for _ in range(4)]
        psT = [self.ps(es, [128, 8, 128], BF16, "psT") for _ in range(2)]
        b_pT = [Buf("pT") for _ in range(2)]
        psO = [self.ps(es, [128, 128], F32, "psO") for _ in range(2)]
        b_pO = [Buf("pO") for _ in range(2)]
        cnt = {"S": 0, "T": 0, "O": 0, "row": 0}
        nq = c.NCL - 3 * c.CCH

        def load_head(i, qc, kc, vc):
            s.dma(s.q_sync, qT[i][:], self.qkvT[qc * 128:(qc + 1) * 128, :], writes=[b_qkv[i]])
            s.dma(s.q_sync, kT[i][:], self.qkvT[kc * 128:(kc + 1) * 128, :], writes=[b_qkv[i]])
            s.dma(s.q_sync, vT[i][:], self.qkvT[vc * 128:(vc + 1) * 128, :], writes=[b_qkv[i]])

        def build_V(i):
            for t0 in range(0, NT, 8):
                tn = min(8, NT - t0)
                P, bP = psT[cnt["T"] % 2], b_pT[cnt["T"] % 2]
                cnt["T"] += 1

                def tr(h):
                    last = None
                    for t in range(tn):
                        last = h.transpose(P[:, t, :], vT[i][:, (t0 + t) * 128:(t0 + t + 1) * 128], self.ident[:])
                    return last

                s.op(s.pe, tr, reads=[b_qkv[i], self.b_const], writes=[bP])
                s.op(s.dve, lambda h: h.tensor_copy(out=Vtm[:, t0:t0 + tn, :], in_=P[:, :tn, :]), reads=[bP], writes=[b_V])

        def transposes_W(r, nt):
            for t0 in range(0, nt, 8):
                tn = min(8, nt - t0)
                P, bP = psT[cnt["T"] % 2], b_pT[cnt["T"] % 2]
                cnt["T"] += 1

                def tr(h):
                    last = None
                    for t in range(tn):
                        last = h.transpose(P[:, t, :], Wb[r][:, (t0 + t) * 128:(t0 + t + 1) * 128], self.ident[:])
                    return last

                s.op(s.pe, tr, reads=[b_Wb[r], self.b_const], writes=[bP])
                s.op(s.act, lambda h: h.copy(out=WT[r][:, t0:t0 + tn, :], in_=P[:, :tn, :]), reads=[bP], writes=[b_WT[r]])

        def kblocks(nk):
            return [(o, min(512, nk - o)) for o in range(0, nk, 512)]

        scale_d = 64 ** -0.5
        scale_s = 128 ** -0.5

        b_pSx = [Buf("pSx") for _ in range(4)]
        tA1 = [Buf("A1") for _ in range(NR)]
        tA2 = [Buf("A2") for _ in range(NR)]
        tA3 = [Buf("A3") for _ in range(NR)]
        tA4 = [Buf("A4") for _ in range(NR)]
        tdg = [Buf("dg") for _ in range(NR)]
        tsa = [Buf("smA") for _ in range(NR)]
        tsd = [Buf("smD") for _ in range(NR)]
        tsc = [Buf("smC") for _ in range(NR)]

        def d_front(i, qi):
            r = qi % NR
            nk = (qi + 1) * 128
            blks = kblocks(nk)
            s.op(s.dve, lambda h: h.memset(sm[r][:], 0.0), writes=[tsa[r], tsd[r], tsc[r]])
            for cm in range(2):
                PA = (A1 if cm == 0 else A2)[r]
                tP = (tA1 if cm == 0 else tA2)[r]
                for bi, (o, w) in enumerate(blks):
                    P, bP = psS[cnt["S"] % 4], b_pS[cnt["S"] % 4]
                    cnt["S"] += 1
                    s.op(s.pe, lambda h: h.matmul(P[:, :w], lhsT=qT[i][cm * 64:(cm + 1) * 64, qi * 128:(qi + 1) * 128],
                                                  rhs=kT[i][cm * 64:(cm + 1) * 64, o:o + w], start=True, stop=True),
                         reads=[b_qkv[i]], writes=[bP])
                    last = (bi == len(blks) - 1)
                    wn = w - 128 if last else w
                    col = cm * 8 + bi

                    def ex(h):
                        ins = None
                        if wn > 0:
                            ins = h.activation(out=PA[:, o:o + wn], in_=P[:, :wn], func=AF.Exp, scale=scale_d,
                                               accum_out=sm[r][:, col:col + 1])
                        if last:
                            ins = h.activation(out=dg[r][:], in_=P[:, wn:w], func=AF.Exp, scale=scale_d)
                        return ins

                    s.op(s.act, ex, reads=[bP], writes=[tP, tsa[r]] + ([tdg[r]] if last else []))
                    if last:
                        s.op(s.dve, lambda h: h.tensor_tensor(out=PA[:, nk - 128:nk], in0=dg[r][:], in1=self.mask_le[:],
                                                              op=ALU.mult), reads=[tdg[r], self.b_const], writes=[tP])
                        s.op(s.dve, lambda h: h.reduce_sum(out=sm[r][:, 24 + cm:25 + cm], in_=PA[:, nk - 128:nk], axis=AX.X),
                             reads=[tP], writes=[tsd[r]])

        def d_back(i, qi):
            r = qi % NR
            nk = (qi + 1) * 128

            def comb_a(h):
                h.reduce_sum(out=sm[r][:, 16:17], in_=sm[r][:, 0:8], axis=AX.X)
                return h.reduce_sum(out=sm[r][:, 17:18], in_=sm[r][:, 8:16], axis=AX.X)

            def comb_b(h):
                h.tensor_tensor(out=S1v[:, qi:qi + 1], in0=sm[r][:, 16:17], in1=sm[r][:, 24:25], op=ALU.add)
                return h.tensor_tensor(out=sm[r][:, 17:18], in0=sm[r][:, 17:18], in1=sm[r][:, 25:26], op=ALU.add)

            s.chain(s.dve, [
                comb_a, comb_b,
                lambda h: h.reciprocal(out=sm[r][:, 18:19], in_=sm[r][:, 17:18]),
                lambda h: h.scalar_tensor_tensor(out=sm[r][:, 20:21], in0=S1v[:, qi:qi + 1], scalar=neglam,
                                                 in1=sm[r][:, 18:19], op0=ALU.mult, op1=ALU.mult),
            ], reads=[b_par, tsa[r], tsd[r]], writes=[tsc[r], b_OQ])
            s.op(s.dve, lambda h: h.scalar_tensor_tensor(out=Wb[r][:, :nk], in0=A2[r][:, :nk], scalar=sm[r][:, 20:21],
                                                         in1=A1[r][:, :nk], op0=ALU.mult, op1=ALU.add),
                 reads=[tA1[r], tA2[r], tsc[r]], writes=[b_Wb[r]])
            transposes_W(r, qi + 1)
            PO, bPO = psO[cnt["O"] % 2], b_pO[cnt["O"] % 2]
            cnt["O"] += 1

            def pv(h):
                last = None
                for kt in range(qi + 1):
                    last = h.matmul(PO[:, :], lhsT=WT[r][:, kt, :], rhs=Vtm[:, kt, :], start=(kt == 0), stop=(kt == qi))
                return last

            s.op(s.pe, pv, reads=[b_WT[r], b_V], writes=[bPO])
            s.op(s.act, lambda h: h.copy(out=OQ[:, qi, :], in_=PO[:, :]), reads=[bPO], writes=[b_OQ])

        def d_finish(i):
            SQ = A1[0].rearrange("p (t d) -> p t d", d=128)
            ONb = Wb[0].rearrange("p (t d) -> p t d", d=128)
            s.op(s.dve, lambda h: h.tensor_tensor(out=SQ[:, :, :], in0=OQ[:, :, :], in1=OQ[:, :, :], op=ALU.mult),
                 reads=[b_OQ], writes=[tA1[0]])
            s.op(s.dve, lambda h: h.reduce_sum(out=S1v[:, 32:32 + NT], in_=SQ[:, :, :], axis=AX.X), reads=[tA1[0]],
                 writes=[b_OQ])
            s.chain(s.dve, [
                lambda h: h.tensor_tensor(out=S1v[:, 64:64 + NT], in0=S1v[:, 0:NT], in1=S1v[:, 0:NT], op=ALU.mult),
                lambda h: h.tensor_scalar(out=S1v[:, 32:32 + NT], in0=S1v[:, 32:32 + NT], scalar1=1.0 / 128, scalar2=None,
                                          op0=ALU.mult),
                lambda h: h.scalar_tensor_tensor(out=S1v[:, 96:96 + NT], in0=S1v[:, 64:64 + NT], scalar=LN_EPS,
                                                 in1=S1v[:, 32:32 + NT], op0=ALU.mult, op1=ALU.add),
            ], writes=[b_OQ])
            s.chain(s.act, [
                lambda h: h.activation(out=S1v[:, 96:96 + NT], in_=S1v[:, 96:96 + NT], func=AF.Ln),
                lambda h: h.activation(out=S1v[:, 96:96 + NT], in_=S1v[:, 96:96 + NT], func=AF.Exp, scale=-0.5),
            ], writes=[b_OQ])

            def norm(h):
                last = None
                for t in range(NT):
                    last = h.scalar_tensor_tensor(out=ONb[:, t, :], in0=OQ[:, t, :], scalar=S1v[:, 96 + t:97 + t], in1=gt[:],
                                                  op0=ALU.mult, op1=ALU.mult)
                return last

            s.op(s.dve, norm, reads=[b_OQ, b_par], writes=[b_Wb[0]])
            for t0 in range(0, NT, 8):
                tn = min(8, NT - t0)
                P, bP = psT[cnt["T"] % 2], b_pT[cnt["T"] % 2]
                cnt["T"] += 1

                def tr(h):
                    last = None
                    for t in range(tn):
                        last = h.transpose(P[:, t, :], ONb[:, t0 + t, :], self.ident[:])
                    return last

                s.op(s.pe, tr, reads=[b_Wb[0], self.b_const], writes=[bP])
                s.op(s.act, lambda h: h.copy(out=OT[i][:, t0 * 128:(t0 + tn) * 128].rearrange("p (t d) -> p t d", d=128),
                                             in_=P[:, :tn, :]), reads=[bP], writes=[b_OT[i]])

        def s_front(i, qi):
            r = qi % NR
            nk = (qi + 1) * 128
            blks = kblocks(nk)
            E, SP, ZS, CS = A1[r], A2[r], A3[r], A4[r]
            for bi, (o, w) in enumerate(blks):
                P, bP = psS[cnt["S"] % 4], b_pS[cnt["S"] % 4]
                cnt["S"] += 1
                s.op(s.pe, lambda h: h.matmul(P[:, :w], lhsT=qT[i][:, qi * 128:(qi + 1) * 128], rhs=kT[i][:, o:o + w],
                                              start=True, stop=True), reads=[b_qkv[i]], writes=[bP])
                bX = b_pSx[(cnt["S"] - 1) % 4]
                s.op(s.act, lambda h: h.activation(out=E[:, o:o + w], in_=P[:, :w], func=AF.Exp, scale=scale_s),
                     reads=[bP], writes=[tA1[r], bX])
                s.op(s.dve, lambda h: h.tensor_scalar(out=ZS[:, o:o + w], in0=P[:, :w], scalar1=scale_s, scalar2=None,
                                                      op0=ALU.mult), reads=[bP], writes=[tA3[r], bX])
            s.op(s.act, lambda h: h.activation(out=SP[:, :nk], in_=E[:, :nk], func=AF.Ln, bias=1.0), reads=[tA1[r]],
                 writes=[tA2[r]])
            s.op(s.dve, lambda h: h.tensor_tensor(out=SP[:, nk - 128:nk], in0=SP[:, nk - 128:nk], in1=self.mask_lt[:],
                                                  op=ALU.mult), reads=[self.b_const], writes=[tA2[r]])
            s.op(s.dve, lambda h: h.tensor_tensor_scan(out=CS[:, :nk], data0=self.ones[:, :nk], data1=SP[:, :nk], initial=0.0,
                                                       op0=ALU.mult, op1=ALU.add),
                 reads=[tA2[r], self.b_const], writes=[tA4[r]])

            def scan_c(h):
                h.tensor_tensor(out=ZS[:, 1:nk], in0=ZS[:, 1:nk], in1=CS[:, 0:nk - 1], op=ALU.add)
                return h.tensor_scalar(out=sm[r][:, 30:31], in0=CS[:, nk - 1:nk], scalar1=-1.0, scalar2=None, op0=ALU.mult)

            s.op(s.dve, scan_c, reads=[tA4[r]], writes=[tA3[r], tsc[r]])

        def s_back(i, qi):
            r = qi % NR
            nk = (qi + 1) * 128
            ZS = A3[r]

            def wexp(h):
                ins = None
                if nk > 128:
                    ins = h.activation(out=Wb[r][:, :nk - 128], in_=ZS[:, :nk - 128], func=AF.Exp, bias=sm[r][:, 30:31])
                return h.activation(out=dg[r][:], in_=ZS[:, nk - 128:nk], func=AF.Exp, bias=sm[r][:, 30:31])

            s.op(s.act, wexp, reads=[tA3[r], tsc[r]], writes=[b_Wb[r], tdg[r]])
            s.op(s.dve, lambda h: h.tensor_tensor(out=Wb[r][:, nk - 128:nk], in0=dg[r][:], in1=self.mask_lt[:], op=ALU.mult),
                 reads=[tdg[r], self.b_const], writes=[b_Wb[r]])
            transposes_W(r, qi + 1)
            PO, bPO = psO[cnt["O"] % 2], b_pO[cnt["O"] % 2]
            cnt["O"] += 1

            def pv(h):
                last = None
                for kt in range(qi + 1):
                    last = h.matmul(PO[:, :], lhsT=Vtm[:, kt, :], rhs=WT[r][:, kt, :], start=(kt == 0), stop=(kt == qi))
                return last

            s.op(s.pe, pv, reads=[b_WT[r], b_V], writes=[bPO])
            s.op(s.act, lambda h: h.copy(out=OT[i][:, qi * 128:(qi + 1) * 128], in_=PO[:, :]), reads=[bPO],
                 writes=[b_OT[i]])

        def run_rows(front, back, i):
            front(i, 0)
            for qi in range(NT):
                if qi + 1 < NT:
                    front(i, qi + 1)
                back(i, qi)

        hi = 0
        for hd in range(c.HD):
            i = hi % NX
            hi += 1
            load_head(i, 0 * c.HD + hd, 1 * c.HD + hd, 2 * c.HD + hd)
            build_V(i)
            run_rows(d_front, d_back, i)
            d_finish(i)
            self.mix_store(OT[i], b_OT[i], c.CCH + hd)
        for hs in range(c.HS):
            i = hi % NX
            hi += 1
            base = 3 * c.HD
            load_head(i, base + 0 * c.HS + hs, base + 1 * c.HS + hs, base + 2 * c.HS + hs)
            build_V(i)
            run_rows(s_front, s_back, i)
            self.mix_store(OT[i], b_OT[i], c.CCH + c.HD + hs)

    def gemm_T(self, es_outer, XT, b_XT, nk, Wd, row0, bl, res_src, first):
        s, c = self.s, self.c
        D, TB = c.D, c.TB
        with ExitStack() as es:
            wb = [self.sb(es, [128, c.KC, 512], BF16, "wT") for _ in range(2)]
            b_w = [Buf("w") for _ in range(2)]
            pb = [self.ps(es, [128, 512], F32, "pT") for _ in range(6)]
            b_p = [Buf("p") for _ in range(6)]
            NS = 4
            rb = [self.sb(es, [128, 512], F32, "rT") for _ in range(NS)]
            yb = [self.sb(es, [128, 512], F32, "yT") for _ in range(NS)]
            b_r = [Buf("r") for _ in range(NS)]
            b_y = [Buf("y") for _ in range(NS)]
            b_dr = {}
            pi = 0
            si = 0
            for nb in range(D // 512):
                W, bW = wb[nb % 2], b_w[nb % 2]
                self.load_w(W, bW, Wd, row0, nk, nb * 512, 512)
                for (off, n) in c.mtiles:
                    P, bP = pb[pi % 6], b_p[pi % 6]
                    pi += 1
                    RB, YB, bR, bY = rb[si % NS], yb[si % NS], b_r[si % NS], b_y[si % NS]
                    si += 1
                    r0 = bl * TB + off
                    s.dma(s.q_sync, RB[:n, :], res_src[r0:r0 + n, nb * 512:(nb + 1) * 512], writes=[bR])

                    def mm(h):
                        last = None
                        for k in range(nk):
                            last = h.matmul(P[:n, :], lhsT=XT[:, k, off:off + n], rhs=W[:, k, :], start=(k == 0),
                                            stop=(k == nk - 1))
                        return last

                    s.op(s.pe, mm, reads=[bW, b_XT], writes=[bP])
                    sc = c.alpha if first else 1.0
                    s.op(s.dve, lambda h: h.scalar_tensor_tensor(out=YB[:n, :], in0=RB[:n, :], scalar=sc, in1=P[:n, :],
                                                                 op0=ALU.mult, op1=ALU.add),
                         reads=[bR, bP], writes=[bY])
                    s.dma(s.q_sync, self.ypre[r0:r0 + n, nb * 512:(nb + 1) * 512], YB[:n, :], reads=[bY])
            s.barrier()

    def stage_wup(self, l, XT, b_XT, XTh, b_XTh, blk):
        s, c = self.s, self.c
        D, KC, TB, DFF, FC = c.D, c.KC, c.TB, c.DFF, c.FC
        with ExitStack() as es:
            fw = self.sb(es, [128, 2 * FC * 3], F32, "fcw")
            b_fw = Buf("fcw")
            s.dma(s.q_sync, fw[:], self.fcw[l * 128:(l + 1) * 128, :], writes=[b_fw])
            wgu = [self.sb(es, [128, KC, 512], BF16, "wgu") for _ in range(2)]
            b_w = [Buf("w") for _ in range(2)]
            pb = [self.ps(es, [128, 512], F32, "pF") for _ in range(6)]
            b_p = [Buf("p") for _ in range(6)]
            ph = [self.ps(es, [128, 16], F32, "pH") for _ in range(2)]
            b_ph = [Buf("ph") for _ in range(2)]
            NS = 2
            ug = [self.sb(es, [128, TB + 2], F32, "ug") for _ in range(NS)]
            uu = [self.sb(es, [128, TB + 2], F32, "uu") for _ in range(NS)]
            cg = [self.sb(es, [128, TB], F32, "cg") for _ in range(NS)]
            cu = [self.sb(es, [128, TB], F32, "cu") for _ in range(NS)]
            ab = [self.sb(es, [128, TB], BF16, "ab") for _ in range(NS)]
            b_ug = [Buf("ug") for _ in range(NS)]
            b_uu = [Buf("uu") for _ in range(NS)]
            b_cg = [Buf("cg") for _ in range(NS)]
            b_cu = [Buf("cu") for _ in range(NS)]
            b_ab = [Buf("ab") for _ in range(NS)]
            pi = 0
            hi = 0
            for g in range(FC // 2):
                WGU, bW = wgu[g % 2], b_w[g % 2]
                self.load_w(WGU, bW, self.w_up, l * D, KC, g * 512, 512)
                for j in range(2):
                    ch = g * 2 + j
                    i = ch % NS
                    for (W, U, bU) in ((WGU[:, :, 0:256], ug[i], b_ug[i]), (WGU[:, :, 256:512], uu[i], b_uu[i])):
                        if blk == 0:
                            s.op(s.dve, lambda h: h.memset(U[:, 0:2], 0.0), writes=[bU])
                        else:
                            PH, bPH = ph[hi % 2], b_ph[hi % 2]
                            hi += 1

                            def mmh(h):
                                last = None
                                for k in range(KC):
                                    last = h.matmul(PH[:, 0:2], lhsT=W[:, k, j * 128:(j + 1) * 128], rhs=XTh[:, k, 0:2],
                                                    start=(k == 0), stop=(k == KC - 1))
                                return last

                            s.op(s.pe, mmh, reads=[bW, b_XTh], writes=[bPH])
                            s.op(s.act, lambda h: h.copy(out=U[:, 0:2], in_=PH[:, 0:2]), reads=[bPH], writes=[bU])
                        for (o, n) in c.tsubs:
                            P, bP = pb[pi % 6], b_p[pi % 6]
                            pi += 1

                            def mm(h):
                                last = None
                                for k in range(KC):
                                    last = h.matmul(P[:, :n], lhsT=W[:, k, j * 128:(j + 1) * 128], rhs=XT[:, k, o:o + n],
                                                    start=(k == 0), stop=(k == KC - 1))
                                return last

                            s.op(s.pe, mm, reads=[bW, b_XT], writes=[bP])
                            s.op(s.act, lambda h: h.copy(out=U[:, 2 + o:2 + o + n], in_=P[:, :n]), reads=[bP], writes=[bU])
                    UG, UU, CG, CU, AB = ug[i], uu[i], cg[i], cu[i], ab[i]
                    wcol = lambda cidx, k: fw[:, cidx * 3 + k:cidx * 3 + k + 1]

                    def cv0(h):
                        h.tensor_scalar(out=CG[:], in0=UG[:, 2:], scalar1=wcol(ch, 2), scalar2=None, op0=ALU.mult)
                        return h.tensor_scalar(out=CU[:], in0=UU[:, 2:], scalar1=wcol(FC + ch, 2), scalar2=None, op0=ALU.mult)

                    def cvk(k):
                        def f(h):
                            h.scalar_tensor_tensor(out=CG[:], in0=UG[:, k:TB + k], scalar=wcol(ch, k), in1=CG[:],
                                                   op0=ALU.mult, op1=ALU.add)
                            return h.scalar_tensor_tensor(out=CU[:], in0=UU[:, k:TB + k], scalar=wcol(FC + ch, k), in1=CU[:],
                                                          op0=ALU.mult, op1=ALU.add)
                        return f

                    s.chain(s.dve, [cv0, cvk(1), cvk(0)], reads=[b_ug[i], b_uu[i], b_fw], writes=[b_cg[i], b_cu[i]])
                    s.op(s.act, lambda h: h.activation(out=CG[:], in_=CG[:], func=AF.Silu), writes=[b_cg[i]])
                    s.op(s.dve, lambda h: h.tensor_tensor(out=AB[:], in0=CG[:], in1=CU[:], op=ALU.mult),
                         reads=[b_cg[i], b_cu[i]], writes=[b_ab[i]])
                    s.dma(s.q_sync, self.aT[ch * 128:(ch + 1) * 128, :], AB[:], reads=[b_ab[i]])
            s.barrier()

    def stage_dense(self, l):
        s, c = self.s, self.c
        D, KC, TB = c.D, c.KC, c.TB
        last_layer = (l == c.DEPTH - 1)
        with ExitStack() as es:
            XT = self.sb(es, [128, KC, TB], BF16, "XTd")
            b_XT = Buf("XT")
            XTh = self.sb(es, [128, KC, 2], BF16, "XTh")
            b_XTh = Buf("XTh")
            for bl in range(c.NBL):
                blk = bl
                s.dma(s.q_sync, XT[:, :, :], self.mixT[blk * c.MCL * 128:(blk + 1) * c.MCL * 128, :]
                      .rearrange("(k p) t -> p k t", p=128), writes=[b_XT])
                self.gemm_T(es, XT, b_XT, KC, self.w_out, l * D, bl, self.hres, True)
                self.ln_pass(es, self.ypre, self.ln1g[l:l + 1, :], self.ln1b[l:l + 1, :], XT, b_XT, bl, h_dst=self.hres)
                self.stage_wup(l, XT, b_XT, XTh, b_XTh, blk)
                if bl + 1 < c.NBL:
                    s.op(s.dve, lambda h: h.tensor_copy(out=XTh[:, :, :], in_=XT[:, :, TB - 2:TB]), reads=[b_XT],
                         writes=[b_XTh])
                    s.barrier()
                for pi, (f0, fn) in enumerate(c.fparts):
                    s.dma(s.q_sync, XT[:, :fn, :], self.aT[f0 * 128:(f0 + fn) * 128, :].rearrange("(k p) t -> p k t", p=128),
                          writes=[b_XT])
                    self.gemm_T(es, XT, b_XT, fn, self.w_down, l * c.DFF + f0 * 128, bl,
                                self.hres if pi == 0 else self.ypre, pi == 0)
                if last_layer:
                    self.ln_pass(es, self.ypre, self.ln2g[l:l + 1, :], self.ln2b[l:l + 1, :], None, None, bl, h_dst=self.out,
                                 dst_off=self.sq * c.NBL * TB)
                else:
                    dst = self.hnT[blk * KC * 128:(blk + 1) * KC * 128, :]
                    self.ln_pass(es, self.ypre, self.ln2g[l:l + 1, :], self.ln2b[l:l + 1, :], XT, b_XT, bl,
                                 h_dst=self.hres, hnT_dst=dst)


def make_in_maps(cfg: Cfg, inp):
    c = cfg
    f = lambda a: np.ascontiguousarray(np.asarray(a, dtype=np.float32))
    x = f(inp["x"])
    meta = f(inp["meta_tokens"])
    D, DEPTH = c.D, c.DEPTH
    rows = c.mix_rows()
    w_out = (f(inp["w_out"]) if c.NPAIR == 1 else f(inp["w_out"])[:, rows, :]).reshape(DEPTH * D, D)
    w_up = np.ascontiguousarray(f(inp["w_up"]).reshape(DEPTH * D, 2, c.FC // 2, 256).transpose(0, 2, 1, 3)
                                .reshape(DEPTH * D, 2 * c.DFF))
    w_down = f(inp["w_down"]).reshape(DEPTH * c.DFF, D)
    fcw = f(inp["ffn_conv_w"]).reshape(DEPTH, 3, 2 * c.FC, 128).transpose(0, 3, 2, 1).reshape(DEPTH * 128, 2 * c.FC * 3)
    lamv = np.stack([f(inp["lambda_q1"]), f(inp["lambda_k1"]), f(inp["lambda_q2"]), f(inp["lambda_k2"])], axis=1)
    lamv = np.ascontiguousarray(lamv.reshape(DEPTH * 4, 64))
    common = {
        "embg": f(inp["emb_ln_g"]).reshape(1, D), "embb": f(inp["emb_ln_b"]).reshape(1, D),
        "lamv": lamv, "dng": f(inp["diff_norm_g"]),
        "w_out": np.ascontiguousarray(w_out), "ln1g": f(inp["ln1_g"]), "ln1b": f(inp["ln1_b"]),
        "w_up": w_up, "fcw": np.ascontiguousarray(fcw), "w_down": w_down,
        "ln2g": f(inp["ln2_g"]), "ln2b": f(inp["ln2_b"]),
    }
    per_rank = []
    for r in range(c.NPAIR):
        cols = c.in_cols(r)
        w_in = np.ascontiguousarray((f(inp["w_in"]) if c.NPAIR == 1 else f(inp["w_in"])[:, :, cols]).reshape(DEPTH * D, c.NCL * 128))
        scw = f(inp["short_conv_w"])[:, :, r * c.CCH * 128:(r + 1) * c.CCH * 128]
        scw = scw.reshape(DEPTH, 3, c.CCH, 128).transpose(0, 3, 2, 1).reshape(DEPTH * 128, c.CCH * 3)
        per_rank.append({"w_in": w_in, "scw": np.ascontiguousarray(scw)})
    maps = []
    for core in range(c.B // c.SPC):
        toks = np.zeros((c.SPC, c.T, D), np.float32)
        for j in range(c.SPC):
            toks[j, :N_META] = meta
            toks[j, N_META:c.L] = x[core * c.SPC + j]
        m = dict(common)
        m.update(per_rank[0])
        m["tok"] = toks.reshape(c.SPC * c.T, D)
        maps.append(m)
    return maps


def gather_out(cfg: Cfg, results):
    c = cfg
    out = np.zeros((c.B, c.SEQ, c.D), np.float32)
    for core in range(c.B // c.SPC):
        rows = np.asarray(results[core]["out"]).reshape(c.SPC, c.T, c.D)
        for j in range(c.SPC):
            out[core * c.SPC + j] = rows[j, N_META:c.L]
    return out


_CACHE = {}


def run(cfg: Cfg, inp):
    key = (cfg.D, cfg.SEQ, cfg.DEPTH, cfg.NPAIR, cfg.B, cfg.SPC)
    if key not in _CACHE:
        _CACHE[key] = Builder(cfg).build()
    nc = _CACHE[key]
    maps = make_in_maps(cfg, inp)
    res = run_bass_kernel_spmd(nc, maps, core_ids=list(range(len(maps))))
    return gather_out(cfg, res.results)


def kernel(**inputs):
    cfg = Cfg(D=4096, SEQ=2048, DEPTH=4, NPAIR=1, B=4, SPC=2)
    return run(cfg, inputs)
```
